# Optimizing a Trainium2 kernel written in Bass

```python
import jax, jax.numpy as jnp
from jax import lax
import numpy as np

D_MODEL = 2048
BATCH = 8
SEQ = 2048
DEPTH = 2

HEAD_DIM = 128
RET_WIDTH = D_MODEL // 2
N_RET_HEADS = RET_WIDTH // HEAD_DIM
FOX_WIDTH = D_MODEL // 2
N_FOX_HEADS = FOX_WIDTH // HEAD_DIM
POOL_WIDTH = D_MODEL // 2
POOL_WINDOWS = (2, 4, 8, 16)
N_POOL_GROUPS = len(POOL_WINDOWS)
POOL_GROUP_DIM = POOL_WIDTH // N_POOL_GROUPS
N_BRANCH = 3
RET_CHUNK = 128
Q_BLOCK = 128
ROPE_THETA = 10000.0
EPS = 1e-6
NEG_INF = -1e30
C_IN = 4 * RET_WIDTH + 4 * FOX_WIDTH + N_FOX_HEADS + 2 * POOL_WIDTH + N_BRANCH * D_MODEL

kernel_name = 'retention_fox_pool_gated_hybrid'


def rms_norm(x, g):
    xf = x.astype(jnp.float32)
    y = xf * lax.rsqrt(jnp.mean(xf * xf, axis=-1, keepdims=True) + EPS)
    return (y * g.astype(jnp.float32)).astype(x.dtype)


def rotary(t, pos):
    half = t.shape[-1] // 2
    inv = ROPE_THETA ** (-jnp.arange(half, dtype=jnp.float32) / half)
    ang = pos.astype(jnp.float32)[:, None] * inv[None, :]
    cos = jnp.cos(ang)[None, :, None, :]
    sin = jnp.sin(ang)[None, :, None, :]
    t1, t2 = t[..., :half], t[..., half:]
    return jnp.concatenate([t1 * cos - t2 * sin, t1 * sin + t2 * cos], axis=-1)


def retention(q, k, v):
    bsz, s, h, dk = q.shape
    dv = v.shape[-1]
    n = s // RET_CHUNK
    pos = jnp.arange(s)
    q = rotary(q, pos)
    k = rotary(k, pos) * (dk ** -0.5)
    log_g = jnp.log1p(-jnp.exp2(-5.0 - jnp.arange(h, dtype=jnp.float32)))
    idx = jnp.arange(RET_CHUNK, dtype=jnp.float32)
    diff = idx[:, None] - idx[None, :]
    decay = jnp.where(diff >= 0, jnp.exp(log_g[:, None, None] * jnp.maximum(diff, 0.0)), 0.0)
    xi = jnp.exp(log_g[:, None] * (idx + 1.0))[None, :, :, None]
    zeta = jnp.exp(log_g[:, None] * (RET_CHUNK - 1.0 - idx))[None, :, :, None]
    g_chunk = jnp.exp(log_g * RET_CHUNK)[None, :, None, None]

    def chunks(t):
        return t.reshape(bsz, n, RET_CHUNK, h, t.shape[-1]).transpose(1, 0, 3, 2, 4)

    def step(state, qkv):
        qc, kc, vc = qkv
        inner = jnp.einsum('bhqd,bhkd->bhqk', qc, kc) * decay
        out = (jnp.einsum('bhqk,bhkv->bhqv', inner, vc)
               + jnp.einsum('bhqd,bhdv->bhqv', qc, state) * xi)
        state = g_chunk * state + jnp.einsum('bhkd,bhkv->bhdv', kc * zeta, vc)
        return state, out

    state0 = jnp.zeros((bsz, h, dk, dv), jnp.float32)
    _, outs = lax.scan(step, state0, (chunks(q), chunks(k), chunks(v)))
    return outs.transpose(1, 0, 3, 2, 4).reshape(bsz, s, h, dv)


def head_group_norm(y, g):
    mu = jnp.mean(y, axis=-1, keepdims=True)
    var = jnp.mean(jnp.square(y - mu), axis=-1, keepdims=True)
    yn = (y - mu) * lax.rsqrt(var + EPS)
    return yn * g.astype(jnp.float32).reshape(y.shape[-2], y.shape[-1])


def forgetting_attention(q, k, v, f_logit):
    bsz, s, h, d = q.shape
    n = s // Q_BLOCK
    scale = d ** -0.5
    c = jnp.cumsum(jax.nn.log_sigmoid(f_logit), axis=1).transpose(0, 2, 1)
    qh = q.transpose(0, 2, 1, 3)
    kh = k.transpose(0, 2, 1, 3)
    vh = v.transpose(0, 2, 1, 3)
    qb = qh.reshape(bsz, h, n, Q_BLOCK, d).transpose(2, 0, 1, 3, 4)
    cb = c.reshape(bsz, h, n, Q_BLOCK).transpose(2, 0, 1, 3)
    qpos = jnp.arange(s).reshape(n, Q_BLOCK)
    kpos = jnp.arange(s)

    def block(args):
        qi, ci, pi = args
        logits = (jnp.einsum('bhqd,bhkd->bhqk', qi, kh) * scale
                  + ci[..., None] - c[:, :, None, :])
        logits = jnp.where(pi[:, None] >= kpos[None, :], logits, NEG_INF)
        p = jax.nn.softmax(logits, axis=-1)
        return jnp.einsum('bhqk,bhkd->bhqd', p, vh)

    out = lax.map(block, (qb, cb, qpos))
    return out.transpose(1, 0, 3, 2, 4).reshape(bsz, s, h, d)


def pool_mixer(u, w_pool, pool_scale):
    bsz, s, p = u.shape
    cs = jnp.cumsum(u, axis=1)
    t = jnp.arange(s, dtype=jnp.float32) + 1.0
    groups = []
    for gi, w in enumerate(POOL_WINDOWS):
        sl = slice(gi * POOL_GROUP_DIM, (gi + 1) * POOL_GROUP_DIM)
        csg = cs[..., sl]
        prev = jnp.pad(csg, ((0, 0), (w, 0), (0, 0)))[:, :s]
        mean = (csg - prev) / jnp.minimum(t, float(w))[None, :, None]
        groups.append(mean - u[..., sl])
    pooled = jnp.stack(groups, axis=2)
    mixed = jnp.einsum('bsgc,gcd->bsgd', pooled, w_pool.astype(jnp.float32)).reshape(bsz, s, p)
    return mixed * pool_scale.astype(jnp.float32)


def hybrid_layer(x, norm_g, w_in, ret_gn_g, fox_b_f, pool_w, pool_scale,
                 w_ret_branch, w_fox_branch, w_pool_branch, w_out):
    bsz, s, _ = x.shape
    h = rms_norm(x, norm_g)
    proj = jnp.matmul(h, w_in).astype(jnp.float32)
    sizes = [RET_WIDTH] * 4 + [FOX_WIDTH] * 4 + [N_FOX_HEADS] + [POOL_WIDTH] * 2 + [D_MODEL] * N_BRANCH
    cuts = np.cumsum(sizes)[:-1].tolist()
    (rq, rk, rv, rz, fq, fk, fv, fz, ff, pu, pz, ga, gb, gc) = jnp.split(proj, cuts, axis=-1)

    hs = (bsz, s, N_RET_HEADS, HEAD_DIM)
    y_ret = retention(rq.reshape(hs), rk.reshape(hs), rv.reshape(hs))
    y_ret = head_group_norm(y_ret, ret_gn_g).reshape(bsz, s, RET_WIDTH) * jax.nn.silu(rz)

    hs = (bsz, s, N_FOX_HEADS, HEAD_DIM)
    f_logit = ff + fox_b_f.astype(jnp.float32)
    y_fox = forgetting_attention(fq.reshape(hs), fk.reshape(hs), fv.reshape(hs), f_logit)
    y_fox = y_fox.reshape(bsz, s, FOX_WIDTH) * jax.nn.silu(fz)

    y_pool = pool_mixer(pu, pool_w, pool_scale) * jax.nn.silu(pz)

    merged = (jax.nn.sigmoid(ga) * jnp.matmul(y_ret, w_ret_branch.astype(jnp.float32))
              + jax.nn.sigmoid(gb) * jnp.matmul(y_fox, w_fox_branch.astype(jnp.float32))
              + jax.nn.sigmoid(gc) * jnp.matmul(y_pool, w_pool_branch.astype(jnp.float32)))
    out = jnp.matmul(merged, w_out.astype(jnp.float32))
    return x + out.astype(x.dtype)


def setup_inputs(seed: int = 0) -> dict:
    key = jax.random.key(seed)
    ks = jax.random.split(key, 13)
    f32 = jnp.float32
    x = jax.random.normal(ks[0], (BATCH, SEQ, D_MODEL), f32)
    norm_g = 1.0 + 0.02 * jax.random.normal(ks[1], (DEPTH, D_MODEL), f32)
    w_in = jax.random.normal(ks[2], (DEPTH, D_MODEL, C_IN), f32) * (D_MODEL ** -0.5)
    ret_gn_g = 1.0 + 0.02 * jax.random.normal(ks[3], (DEPTH, RET_WIDTH), f32)
    fox_b_f = 2.0 + 0.5 * jax.random.normal(ks[4], (DEPTH, N_FOX_HEADS), f32)
    pool_w = jax.random.normal(ks[5], (DEPTH, N_POOL_GROUPS, POOL_GROUP_DIM, POOL_GROUP_DIM), f32) * (POOL_GROUP_DIM ** -0.5)
    pool_scale = 1.0 + 0.02 * jax.random.normal(ks[6], (DEPTH, POOL_WIDTH), f32)
    w_ret_branch = jax.random.normal(ks[7], (DEPTH, RET_WIDTH, D_MODEL), f32) * (RET_WIDTH ** -0.5)
    w_fox_branch = jax.random.normal(ks[8], (DEPTH, FOX_WIDTH, D_MODEL), f32) * (FOX_WIDTH ** -0.5)
    w_pool_branch = jax.random.normal(ks[9], (DEPTH, POOL_WIDTH, D_MODEL), f32) * (POOL_WIDTH ** -0.5)
    w_out = jax.random.normal(ks[10], (DEPTH, D_MODEL, D_MODEL), f32) * (D_MODEL ** -0.5)
    final_g = 1.0 + 0.02 * jax.random.normal(ks[11], (D_MODEL,), f32)
    return {'x': x, 'norm_g': norm_g, 'w_in': w_in, 'ret_gn_g': ret_gn_g, 'fox_b_f': fox_b_f,
            'pool_w': pool_w, 'pool_scale': pool_scale, 'w_ret_branch': w_ret_branch,
            'w_fox_branch': w_fox_branch, 'w_pool_branch': w_pool_branch, 'w_out': w_out,
            'final_g': final_g}


def reference(x, norm_g, w_in, ret_gn_g, fox_b_f, pool_w, pool_scale,
              w_ret_branch, w_fox_branch, w_pool_branch, w_out, final_g):
    for layer in range(DEPTH):
        x = hybrid_layer(x, norm_g[layer], w_in[layer], ret_gn_g[layer], fox_b_f[layer],
                         pool_w[layer], pool_scale[layer], w_ret_branch[layer],
                         w_fox_branch[layer], w_pool_branch[layer], w_out[layer])
    return rms_norm(x, final_g)
```

```python
import numpy as np
import ml_dtypes
import concourse.bass as bass
import concourse.mybir as mybir
from concourse.bass_utils import run_bass_kernel_spmd

F32 = mybir.dt.float32
BF16 = mybir.dt.bfloat16
AF = mybir.ActivationFunctionType
ALU = mybir.AluOpType
AX = mybir.AxisListType

S = 2048
D = 2048
KC = 16
NH = 8
C_IN = 16392
RQ, RK, RV, RZ, FQ, FK, FV, FZ, FF, PU, PZ, GA, GB, GC = (
    0, 1024, 2048, 3072, 4096, 5120, 6144, 7168, 8192, 8200, 9224, 10248, 12296, 14344)
EPS = 1e-6
POOL_W = (2, 4, 8, 16)


class Buf:
    __slots__ = ('w', 'rc', 'rd', 'name', 'excl')

    def __init__(self, name='', excl=False):
        self.w = None
        self.rc = {}
        self.rd = []
        self.name = name
        self.excl = excl


class Op:
    __slots__ = ('stream', 'idx', 'fn', 'waits', 'dma', 'sem', 'semval', 'marked', 'cnt')

    def __init__(self, stream, idx, fn, dma):
        self.stream = stream
        self.idx = idx
        self.fn = fn
        self.dma = dma
        self.waits = []
        self.sem = None
        self.semval = 0
        self.marked = False
        self.cnt = 0


class Prog:
    STREAMS = ('pe', 'act', 'dve', 'pool', 'sp')

    def __init__(self, nc, ndma_sems=None):
        self.nc = nc
        self.ops = {s: [] for s in self.STREAMS}
        self.waited = {s: {p: -1 for p in self.STREAMS} for s in self.STREAMS}
        self.waited_dma = {s: {} for s in self.STREAMS}
        self.csem = {s: nc.alloc_semaphore(name='c_' + s) for s in ('pe', 'act', 'dve', 'pool')}
        nd = ndma_sems or {'sp': 12, 'act': 6, 'pool': 10}
        self.dsems = {s: [nc.alloc_semaphore(name='d_%s%d' % (s, i)) for i in range(n)] for s, n in nd.items()}
        self.dcount = {s: 0 for s in nd}
        self.dhist = {s: [] for s in nd}
        self.pending = {s: [] for s in self.STREAMS}

    def _wait_on(self, o, p, kind):
        if p is o:
            return
        if p.dma:
            key = id(p.sem)
            if self.waited_dma[o.stream].get(key, 0) >= p.semval:
                return
            self.waited_dma[o.stream][key] = p.semval
            o.waits.append(p)
            return
        if p.stream == o.stream and not o.dma:
            if p.stream == 'pe':
                return
        if self.waited[o.stream][p.stream] >= p.idx:
            return
        self.waited[o.stream][p.stream] = p.idx
        p.marked = True
        o.waits.append(p)

    def op(self, stream, fn, reads=(), writes=(), dma=False):
        o = Op(stream, len(self.ops[stream]), fn, dma)
        ex = [b for b in reads if b.excl]
        if ex:
            writes = list(writes) + ex
            reads = [b for b in reads if not b.excl]
        if self.pending[stream]:
            for p in self.pending[stream]:
                self._wait_on(o, p, 'raw')
            self.pending[stream] = []
        if dma:
            k = self.dcount[stream]
            sems = self.dsems[stream]
            n = len(sems)
            o.sem = sems[k % n]
            o.semval = 16 * (k // n + 1)
            if k >= n:
                self._wait_on(o, self.dhist[stream][k - n], 'sem')
            self.dcount[stream] = k + 1
            self.dhist[stream].append(o)
        for b in reads:
            if b.w is not None:
                self._wait_on(o, b.w, 'raw')
        for b in writes:
            if b.w is not None:
                self._wait_on(o, b.w, 'waw')
            for r in b.rc.values():
                self._wait_on(o, r, 'war')
            for r in b.rd:
                self._wait_on(o, r, 'war')
        for b in reads:
            if dma:
                b.rd.append(o)
            else:
                b.rc[stream] = o
        for b in writes:
            b.w = o
            b.rc = {}
            b.rd = []
        self.ops[stream].append(o)
        return o

    def barrier(self):
        tails = []
        for s in self.STREAMS:
            lst = self.ops[s]
            for o in reversed(lst):
                if not o.dma:
                    tails.append(o)
                    break
        for s in self.dhist:
            h = self.dhist[s]
            n = len(self.dsems[s])
            tails.extend(h[-n:])
        for s in self.STREAMS:
            self.pending[s] = list(tails)

    def pe(self, fn, reads=(), writes=()):
        return self.op('pe', fn, reads, writes)

    def act(self, fn, reads=(), writes=()):
        return self.op('act', fn, reads, writes)

    def dve(self, fn, reads=(), writes=()):
        return self.op('dve', fn, reads, writes)

    def pool(self, fn, reads=(), writes=()):
        return self.op('pool', fn, reads, writes)

    def dma(self, stream, out, in_, reads=(), writes=()):
        return self.op(stream, lambda e: e.dma_start(out=out, in_=in_), reads, writes, dma=True)

    def emit(self, final_waits=()):
        nc = self.nc
        for s in self.STREAMS:
            c = 0
            for o in self.ops[s]:
                if o.marked and not o.dma:
                    c += 1
                o.cnt = c
        csem = self.csem

        def run(s, eng, extra=()):
            for o in self.ops[s]:
                for p in o.waits:
                    if p.dma:
                        eng.wait_ge(p.sem, p.semval)
                    else:
                        eng.wait_ge(csem[p.stream], p.cnt)
                ins = o.fn(eng)
                if o.dma:
                    ins.then_inc(o.sem, 16)
                elif o.marked:
                    ins.then_inc(csem[s], 1)
            for p in extra:
                eng.wait_ge(p.sem, p.semval)

        with nc.Block() as block:
            @block.tensor
            def _(eng):
                run('pe', eng)

            @block.scalar
            def _(eng):
                run('act', eng)

            @block.vector
            def _(eng):
                run('dve', eng)

            @block.gpsimd
            def _(eng):
                run('pool', eng)

            @block.sync
            def _(eng):
                run('sp', eng, final_waits)


def make_consts():
    c = {}
    c['cidn'] = np.eye(128, dtype=np.float32)
    p = np.arange(128)
    negm = np.where(p[:, None] > p[None, :], -30000.0, 0.0).astype(np.float32)
    cb = np.concatenate([np.eye(128, dtype=np.float32), np.ones((128, 128), np.float32), negm], axis=1)
    c['cbf'] = cb.astype(ml_dtypes.bfloat16)
    half = 64
    inv = (np.float32(10000.0) ** (-(np.arange(half, dtype=np.float32)) / np.float32(half))).astype(np.float32)
    pos = np.arange(S, dtype=np.float32)
    ang = (pos[None, :] * inv[:, None]).astype(np.float32)
    cs = np.cos(ang).astype(np.float32)
    sn = np.sin(ang).astype(np.float32)
    c['ccos'] = np.concatenate([cs, cs], axis=0)
    c['csin'] = np.concatenate([-sn, sn], axis=0)
    hh = np.arange(NH, dtype=np.float64)
    log_g = np.log1p(-np.exp2(-5.0 - hh))
    scale = 128.0 ** -0.5
    idx = np.arange(128, dtype=np.float64)
    diff = idx[None, :] - idx[:, None]
    dec = np.where(diff[None] >= 0, np.exp(log_g[:, None, None] * np.maximum(diff[None], 0.0)), 0.0) * scale
    c['cdecay'] = np.ascontiguousarray(dec.transpose(1, 0, 2)).astype(np.float32)
    xi = np.exp(log_g[:, None] * (idx[None, :] + 1.0))
    c['cxi'] = np.ascontiguousarray(np.broadcast_to(xi[None], (128, NH, 128))).astype(np.float32)
    zeta = np.exp(log_g[:, None] * (127.0 - idx[None, :])) * scale
    c['czeta'] = np.ascontiguousarray(zeta.T).astype(np.float32)
    r16 = np.zeros((4, 16), np.float64)
    for g, w in enumerate(POOL_W):
        r16[g] = 1.0 / np.minimum(np.arange(16) + 1.0, float(w))
    c['cr16'] = np.ascontiguousarray(np.broadcast_to(r16[None], (128, 4, 16))).astype(np.float32)
    gch = [float(np.exp(log_g[h] * 128.0)) for h in range(NH)]
    return c, gch


def build(n_layers=2, debug=False, upto='d'):
    nc = bass.Bass("TRN2", target_bir_lowering=False)
    consts, GCH = make_consts()
    NL = n_layers

    def din(name, shape, dt=F32):
        if upto == 'a' and name in ('w_in', 'pool_w', 'w_ret_branch', 'w_fox_branch', 'w_pool_branch', 'w_out'):
            shape = [NL, 1, 1]
        return nc.dram_tensor(name, list(shape), dt, kind="ExternalInput").ap()

    x_in = din("x", [S, D])
    norm_g = din("norm_g", [NL, 128, KC])
    w_in = din("w_in", [NL, D, C_IN])
    gn_g = din("ret_gn_g", [NL, 128, NH])
    fox_b = din("fox_b_f", [NH, NL])
    pool_w = din("pool_w", [NL, 4, 256, 256])
    pool_s = din("pool_scale", [NL, 128, 8])
    w_rb = din("w_ret_branch", [NL, 1024, D])
    w_fb = din("w_fox_branch", [NL, 1024, D])
    w_pb = din("w_pool_branch", [NL, 1024, D])
    w_out = din("w_out", [NL, D, D])
    final_g = din("final_g", [1, D])
    cidn = din("cidn", [128, 128])
    cbf = din("cbf", [128, 384], BF16)
    ccos = din("ccos", [128, S])
    csin = din("csin", [128, S])
    cdecay = din("cdecay", [128, NH, 128])
    cxi = din("cxi", [128, NH, 128])
    czeta = din("czeta", [128, NH])
    cr16 = din("cr16", [128, 4, 16])
    out = nc.dram_tensor("out", [S, D], F32, kind="ExternalOutput").ap()
    x1 = nc.dram_tensor("x1s", [S, D], F32).ap()
    Yd = nc.dram_tensor("Yd", [3072, S], BF16, kind=("ExternalOutput" if debug else "Internal")).ap()
    Md = nc.dram_tensor("Md", [D, S], BF16).ap()

    P = Prog(nc)

    HT = nc.alloc_sbuf_tensor("HT", [128, KC, S], BF16)
    WB = [nc.alloc_sbuf_tensor("WB%d" % i, [128, 9216], BF16) for i in range(2)]
    AR = nc.alloc_sbuf_tensor("AR", [128, 24576], F32)
    IDN = nc.alloc_sbuf_tensor("IDN", [128, 128], F32)
    CBF = nc.alloc_sbuf_tensor("CBF", [128, 384], BF16)
    ONESF = nc.alloc_sbuf_tensor("ONESF", [128, 128], F32)
    ZETA = nc.alloc_sbuf_tensor("ZETA", [128, NH], F32)
    R16 = nc.alloc_sbuf_tensor("R16", [128, 4, 16], F32)
    GCOL = nc.alloc_sbuf_tensor("GCOL", [128, NL, KC], F32)
    GNG = nc.alloc_sbuf_tensor("GNG", [128, NL, NH], F32)
    PSC = nc.alloc_sbuf_tensor("PSC", [128, NL, 8], F32)
    FBC = nc.alloc_sbuf_tensor("FBC", [NH, NL], F32)
    WFF = nc.alloc_sbuf_tensor("WFF", [128, KC, NH], BF16)
    PW = nc.alloc_sbuf_tensor("PW", [128, 2, 256], BF16)
    CT = nc.alloc_sbuf_tensor("CT", [128, 16, NH], F32)
    CREF = nc.alloc_sbuf_tensor("CREF", [128, NH, 16], F32)
    CD = nc.alloc_sbuf_tensor("CD", [NH, NH, 16], F32)
    STAT = nc.alloc_sbuf_tensor("STAT", [128, 96], F32)
    PS = [nc.alloc_psum_tensor("PS%d" % i, [128, 512], F32) for i in range(8)]
    bPS = [Buf('ps%d' % i, excl=True) for i in range(8)]
    IDB = CBF[:, 0:128]
    ONEB = CBF[:, 128:256]
    NEGM = CBF[:, 256:384]

    def arf(off, n):
        return AR[:, off:off + n]

    def arb(off, n):
        return AR[:, off:off + n].bitcast(BF16)

    bHT = [[Buf('hta%d' % i), Buf('htd%d' % i)] for i in range(4)]
    bWB = [Buf('wb0'), Buf('wb1')]
    bC = {k: Buf(k) for k in ('idn', 'cbf', 'onesf', 'zeta', 'r16', 'gcol', 'gng', 'psc', 'fbc', 'wff', 'pw',
                              'ct', 'cref', 'cd', 'stat', 'cos', 'sin', 'decay', 'xi')}
    bX1 = [Buf() for _ in range(16)]
    bYd = [Buf() for _ in range(24)]
    bMd = [[Buf() for _ in range(2)] for _ in range(16)]
    wcnt = [0]

    P.dma('sp', IDN[:], cidn, writes=[bC['idn']])
    P.dma('sp', CBF[:], cbf, writes=[bC['cbf']])
    P.dma('sp', ZETA[:], czeta, writes=[bC['zeta']])
    P.dma('sp', R16[:], cr16, writes=[bC['r16']])
    P.dma('sp', GCOL[:], norm_g.rearrange("l p k -> p l k"), writes=[bC['gcol']])
    P.dma('sp', GNG[:], gn_g.rearrange("l p k -> p l k"), writes=[bC['gng']])
    P.dma('sp', PSC[:], pool_s.rearrange("l p k -> p l k"), writes=[bC['psc']])
    P.dma('sp', FBC[:], fox_b, writes=[bC['fbc']])
    P.dve(lambda e: e.memset(ONESF[:], 1.0), writes=[bC['onesf']])

    def evac_copy(i, out_ap, in_ap, reads, writes):
        if i % 2 == 0:
            P.act(lambda e: e.copy(out=out_ap, in_=in_ap), reads, writes)
        else:
            P.dve(lambda e: e.tensor_copy(out=out_ap, in_=in_ap), reads, writes)

    def wb_next():
        i = wcnt[0] % 2
        wcnt[0] += 1
        return i

    def load_w4(l, cols):
        i = wb_next()
        v = WB[i][:, 0:8192].rearrange("p (k c) -> p k c", k=KC)
        for j, c0 in enumerate(cols):
            src = w_in[l, :, c0:c0 + 128].rearrange("(k p) c -> p k c", p=128)
            P.dma('pool', v[:, :, j * 128:(j + 1) * 128], src, writes=[bWB[i]])
        return i, v

    def proj_steps(wi, wv, j, evac):
        for k in range(KC):
            for tt in range(4):
                P.pe(lambda e, k=k, tt=tt: e.matmul(PS[tt][:, :], lhsT=wv[:, k, j * 128:(j + 1) * 128],
                                                    rhs=HT[:, k, tt * 512:(tt + 1) * 512],
                                                    start=(k == 0), stop=(k == KC - 1)),
                     reads=[bWB[wi]] + bHT[tt], writes=[bPS[tt]])
            yield
        for tt in range(4):
            evac(tt, PS[tt][:, :], bPS[tt])
        yield

    class Stager:
        def __init__(self, base, nslots):
            self.slots = [arf(base + i * 2048, 2048) for i in range(nslots)]
            self.bufs = [Buf() for _ in range(nslots)]
            self.cnt = 0

        def load(self, dst, src, K, wbufs):
            s = self.cnt % len(self.slots)
            self.cnt += 1
            st = self.slots[s][:, 0:K * 128].rearrange("p (k c) -> p k c", k=K)
            P.dma('sp', st, src, writes=[self.bufs[s]])
            P.pool(lambda e: e.tensor_copy(out=dst, in_=st), reads=[self.bufs[s]], writes=wbufs)

    def load_w4s(stg, l, cols):
        i = wb_next()
        v = WB[i][:, 0:8192].rearrange("p (k c) -> p k c", k=KC)
        for j, c0 in enumerate(cols):
            src = w_in[l, :, c0:c0 + 128].rearrange("(k p) c -> p k c", p=128)
            stg.load(v[:, :, j * 128:(j + 1) * 128], src, KC, [bWB[i]])
        return i, v

    def proj(wi, wv, j, evac):
        for k in range(KC):
            for tt in range(4):
                P.pe(lambda e, k=k, tt=tt: e.matmul(PS[tt][:, :], lhsT=wv[:, k, j * 128:(j + 1) * 128],
                                                    rhs=HT[:, k, tt * 512:(tt + 1) * 512],
                                                    start=(k == 0), stop=(k == KC - 1)),
                     reads=[bWB[wi]] + bHT[tt], writes=[bPS[tt]])
        for tt in range(4):
            evac(tt, PS[tt][:, :], bPS[tt])

    def rms_stats(xt, bx, sq, bsq, col):
        P.pool(lambda e: e.tensor_tensor(out=sq, in0=xt, in1=xt, op=ALU.mult), reads=[bx], writes=[bsq])
        P.dve(lambda e: e.reduce_sum(out=STAT[:, col:col + 1], in_=sq, axis=AX.X), reads=[bsq], writes=[bC['stat']])
        P.dve(lambda e: e.tensor_scalar(out=STAT[:, col + 1:col + 2], in0=STAT[:, col:col + 1], scalar1=1.0 / D,
                                        scalar2=EPS, op0=ALU.mult, op1=ALU.add),
              reads=[bC['stat']], writes=[bC['stat']])
        P.act(lambda e: e.activation(out=STAT[:, col + 2:col + 3], in_=STAT[:, col + 1:col + 2], func=AF.Ln),
              reads=[bC['stat']], writes=[bC['stat']])
        P.act(lambda e: e.activation(out=STAT[:, col + 2:col + 3], in_=STAT[:, col + 2:col + 3], func=AF.Exp, scale=-0.5),
              reads=[bC['stat']], writes=[bC['stat']])
        return STAT[:, col + 2:col + 3]

    def phase_a(l, xsrc, bxsrc):
        xin = [arf(0, 2048), arf(2048, 2048)]
        xn = [arf(4096, 2048), arf(6144, 2048)]
        sq = arf(8192, 2048)
        bxin = [Buf(), Buf()]
        bxn = [Buf(), Buf()]
        bsq = Buf()
        for t in range(16):
            i = t % 2
            P.dma('sp', xin[i], xsrc[t * 128:(t + 1) * 128, :], reads=([bxsrc[t]] if bxsrc else []), writes=[bxin[i]])
            rstd = rms_stats(xin[i], bxin[i], sq, bsq, 4 * i)
            P.act(lambda e, i=i, rstd=rstd: e.mul(out=xn[i], in_=xin[i], mul=rstd),
                  reads=[bxin[i], bC['stat']], writes=[bxn[i]])
            for g4 in range(4):
                pb = 4 + (g4 % 2) + 2 * (t % 2)
                for j in range(4):
                    k = g4 * 4 + j
                    P.pe(lambda e, i=i, k=k, j=j, pb=pb: e.transpose(PS[pb][:, j * 128:(j + 1) * 128],
                                                                     xn[i][:, k * 128:(k + 1) * 128], IDN[:]),
                         reads=[bxn[i], bC['idn']], writes=[bPS[pb]])
                for j in range(4):
                    k = g4 * 4 + j
                    o_ap = HT[:, k, t * 128:(t + 1) * 128]
                    i_ap = PS[pb][:, j * 128:(j + 1) * 128]
                    g_ap = GCOL[:, l, k:k + 1]
                    if g4 % 2 == 0:
                        P.act(lambda e, o_ap=o_ap, i_ap=i_ap, g_ap=g_ap: e.mul(out=o_ap, in_=i_ap, mul=g_ap),
                              reads=[bPS[pb], bC['gcol']], writes=[bHT[t // 4][0]])
                    else:
                        P.dve(lambda e, o_ap=o_ap, i_ap=i_ap, g_ap=g_ap: e.tensor_scalar(
                            out=o_ap, in0=i_ap, scalar1=g_ap, scalar2=None, op0=ALU.mult),
                            reads=[bPS[pb], bC['gcol']], writes=[bHT[t // 4][1]])

    OFF_COS, OFF_SIN, OFF_DEC, OFF_XI, OFF_B = 0, 2048, 4096, 5120, 6144
    COS = arf(OFF_COS, 2048)
    SIN = arf(OFF_SIN, 2048)
    DEC = arf(OFF_DEC, 1024).rearrange("p (h q) -> p h q", h=NH)
    XI = arf(OFF_XI, 1024).rearrange("p (h q) -> p h q", h=NH)

    def store_y(yT, by, chunk):
        P.dma('sp', Yd[chunk * 128:(chunk + 1) * 128, :], yT, reads=[by], writes=[bYd[chunk]])

    def phase_fox(l):
        fstg = Stager(0, 3)
        o = OFF_B
        sets = []
        for s_ in range(2):
            d = {'qT': arb(o, 1024), 'kT': arb(o + 1024, 1024),
                 'V': arb(o + 2048, 1024).rearrange("p (c d) -> p c d", c=16),
                 'sz': arf(o + 3072, 2048)}
            d['b'] = {k: Buf() for k in ('qT', 'kT', 'V', 'sz')}
            sets.append(d)
            o += 5120
        vT = [arf(o, 512), arf(o + 512, 512)]
        bvT = [Buf(), Buf()]
        o += 1024
        yT = arb(o, 1024)
        byT = Buf()
        o += 1024
        PP = [arb(o, 256), arb(o + 256, 256)]
        bPP = [Buf(), Buf()]
        o += 512
        RS = arf(o, 512)
        TM = arf(o + 512, 512)
        bRS, bTM = Buf(), Buf()
        o += 1024
        BIAS = arf(o, 256).rearrange("p (a b) -> p a b", a=16)
        bBIAS = Buf()
        o += 256
        LG = AR[0:NH, o:o + 2048]
        CC = AR[0:NH, o + 2048:o + 4096]
        bLG, bCC = Buf(), Buf()
        o += 4096
        assert o <= 24576

        src = w_in[l, :, FF:FF + NH].rearrange("(k p) c -> p k c", p=128)
        P.dma('pool', WFF[:], src, writes=[bC['wff']])
        for k in range(KC):
            for tt in range(4):
                P.pe(lambda e, k=k, tt=tt: e.matmul(PS[tt][0:NH, :], lhsT=WFF[:, k, :], rhs=HT[:, k, tt * 512:(tt + 1) * 512],
                                                    start=(k == 0), stop=(k == KC - 1)),
                     reads=[bC['wff']] + bHT[tt], writes=[bPS[tt]])
        for tt in range(4):
            P.act(lambda e, tt=tt: e.activation(out=LG[:, tt * 512:(tt + 1) * 512], in_=PS[tt][0:NH, :], func=AF.Sigmoid,
                                                bias=FBC[:, l:l + 1], scale=1.0),
                  reads=[bPS[tt], bC['fbc']], writes=[bLG])
        P.act(lambda e: e.activation(out=LG, in_=LG, func=AF.Ln), reads=[bLG], writes=[bLG])
        P.dve(lambda e: e.tensor_tensor_scan(out=CC, data0=ONESF[0:NH, 0:1].to_broadcast([NH, S]), data1=LG,
                                             initial=0.0, op0=ALU.mult, op1=ALU.add),
              reads=[bLG, bC['onesf']], writes=[bCC])
        for kb in range(16):
            P.pe(lambda e, kb=kb: e.transpose(PS[4][:, kb * NH:(kb + 1) * NH], CC[:, kb * 128:(kb + 1) * 128], IDN[0:NH, 0:NH]),
                 reads=[bCC, bC['idn']], writes=[bPS[4]])
        P.dve(lambda e: e.tensor_copy(out=CT[:].rearrange("p a b -> p (a b)"), in_=PS[4][:, 0:128]),
              reads=[bPS[4]], writes=[bC['ct']])
        cmid = CC.rearrange("h (q t) -> h q t", t=128)[:, :, 64:65].rearrange("h q o -> h o q")
        P.dve(lambda e: e.tensor_tensor(out=CD[:], in0=cmid.to_broadcast([NH, NH, 16]),
                                        in1=IDN[0:NH, 0:NH].unsqueeze(2).to_broadcast([NH, NH, 16]), op=ALU.mult),
              reads=[bCC, bC['idn']], writes=[bC['cd']])
        P.pe(lambda e: e.matmul(PS[5][:, 0:128], lhsT=ONESF[0:NH, :], rhs=CD[:].rearrange("h a b -> h (a b)"),
                                start=True, stop=True),
             reads=[bC['onesf'], bC['cd']], writes=[bPS[5]])
        P.dve(lambda e: e.tensor_copy(out=CREF[:].rearrange("p a b -> p (a b)"), in_=PS[5][:, 0:128]),
              reads=[bPS[5]], writes=[bC['cref']])

        def fox_proj(h):
            st = sets[h % 2]
            b = st['b']
            wi, wv = load_w4s(fstg, l, [FQ + h * 128, FK + h * 128, FV + h * 128, FZ + h * 128])

            def ev_q(tt, ps, bps):
                evac_copy(tt, st['qT'][:, tt * 512:(tt + 1) * 512], ps, [bps], [b['qT']])

            def ev_k(tt, ps, bps):
                evac_copy(tt + 1, st['kT'][:, tt * 512:(tt + 1) * 512], ps, [bps], [b['kT']])

            def ev_v(tt, ps, bps):
                i = tt % 2
                evac_copy(tt, vT[i], ps, [bps], [bvT[i]])
                pb = tt
                for j in range(4):
                    P.pe(lambda e, j=j: e.transpose(PS[pb][:, j * 128:(j + 1) * 128], vT[i][:, j * 128:(j + 1) * 128], IDN[:]),
                         reads=[bvT[i], bC['idn']], writes=[bPS[pb]])
                evac_copy(tt + 1, st['V'][:, tt * 4:(tt + 1) * 4, :].rearrange("p c d -> p (c d)"), PS[pb][:, :],
                          [bPS[pb]], [b['V']])

            def ev_z(tt, ps, bps):
                P.act(lambda e: e.activation(out=st['sz'][:, tt * 512:(tt + 1) * 512], in_=ps, func=AF.Silu),
                      reads=[bps], writes=[b['sz']])

            for part, ev in enumerate((ev_q, ev_k, ev_v, ev_z)):
                for _ in proj_steps(wi, wv, part, ev):
                    yield

        def pull(gen, n):
            if gen is None:
                return
            for _ in range(n):
                try:
                    next(gen)
                except StopIteration:
                    return

        def fox_attn(h, gen):
            st = sets[h % 2]
            b = st['b']
            P.dve(lambda e: e.tensor_tensor(out=BIAS, in0=CREF[:, h, :].unsqueeze(1).to_broadcast([128, 16, 16]),
                                            in1=CT[:, :, h].unsqueeze(2).to_broadcast([128, 16, 16]), op=ALU.subtract),
                  reads=[bC['cref'], bC['ct']], writes=[bBIAS])
            scale = 128.0 ** -0.5
            cnt = [0]
            def do_qt(qt):
                nkb = 4 * qt + 4
                q0 = qt * 512

                def s_mm(kb):
                    sb = 4 + (cnt[0] + kb) % 2
                    lo = max(0, kb - 4 * qt) * 128
                    diag = kb >= 4 * qt
                    P.pe(lambda e: e.matmul(PS[sb][:, lo:512], lhsT=st['kT'][:, kb * 128:(kb + 1) * 128],
                                            rhs=st['qT'][:, q0 + lo:q0 + 512], start=True, stop=not diag),
                         reads=[b['kT'], b['qT']], writes=[bPS[sb]])
                    if diag:
                        P.pe(lambda e: e.matmul(PS[sb][:, lo:lo + 128], lhsT=IDB, rhs=NEGM, start=False, stop=True),
                             reads=[bC['cbf']], writes=[bPS[sb]])

                def p_exp(kb):
                    sb = 4 + (cnt[0] + kb) % 2
                    pi = (cnt[0] + kb) % 2
                    lo = max(0, kb - 4 * qt)
                    for j in range(lo, 4):
                        qb = qt * 4 + j
                        P.act(lambda e, j=j, qb=qb: e.activation(out=PP[pi][:, j * 128:(j + 1) * 128],
                                                                 in_=PS[sb][:, j * 128:(j + 1) * 128], func=AF.Exp,
                                                                 bias=BIAS[:, kb, qb:qb + 1], scale=scale),
                              reads=[bPS[sb], bBIAS], writes=[bPP[pi]])

                def pv_mm(kb):
                    pi = (cnt[0] + kb) % 2
                    lo = max(0, kb - 4 * qt) * 128
                    P.pe(lambda e: e.matmul(PS[6][:, lo:512], lhsT=st['V'][:, kb, :], rhs=PP[pi][:, lo:512],
                                            start=(kb == 0), stop=(kb == nkb - 1)),
                         reads=[b['V'], bPP[pi]], writes=[bPS[6]])
                    P.pe(lambda e: e.matmul(PS[7][:, lo:512], lhsT=ONEB, rhs=PP[pi][:, lo:512],
                                            start=(kb == 0), stop=(kb == nkb - 1)),
                         reads=[bC['cbf'], bPP[pi]], writes=[bPS[7]])

                s_mm(0)
                for kb in range(nkb):
                    if kb + 1 < nkb:
                        s_mm(kb + 1)
                    p_exp(kb)
                    pull(gen, 2)
                    pv_mm(kb)
                cnt[0] += nkb
                P.dve(lambda e: e.reciprocal(out=RS, in_=PS[7][:, :]), reads=[bPS[7]], writes=[bRS])
                P.dve(lambda e: e.tensor_tensor(out=TM, in0=PS[6][:, :], in1=RS, op=ALU.mult),
                      reads=[bPS[6], bRS], writes=[bTM])
                P.dve(lambda e: e.tensor_tensor(out=yT[:, q0:q0 + 512], in0=TM, in1=st['sz'][:, q0:q0 + 512], op=ALU.mult),
                      reads=[bTM, b['sz']], writes=[byT])
            for qt in range(4):
                do_qt(qt)
            store_y(yT, byT, 8 + h)

        pull(fox_proj(0), 10 ** 6)
        for h in range(NH):
            gen = fox_proj(h + 1) if h + 1 < NH else None
            fox_attn(h, gen)
            pull(gen, 10 ** 6)

    def phase_ret(l):
        o = OFF_B
        QF = [arf(o, 512), arf(o + 512, 512)]
        QS = [arf(o + 1024, 512), arf(o + 1536, 512)]
        bQF, bQS = [Buf(), Buf()], [Buf(), Buf()]
        o += 2048
        qT = arb(o, 1024)
        qxT = arb(o + 1024, 1024)
        kT = arb(o + 2048, 1024)
        o += 3072
        KZ = arb(o, 1024).rearrange("p (c d) -> p c d", c=16)
        V = arb(o + 1024, 1024).rearrange("p (c d) -> p c d", c=16)
        o += 2048
        vT = [arf(o, 512), arf(o + 512, 512)]
        bvT = [Buf(), Buf()]
        o += 1024
        SZ = arf(o, 2048)
        o += 2048
        SF = arf(o, 2048).rearrange("p (c d) -> p c d", c=16)
        o += 2048
        SB = arb(o, 1024).rearrange("p (c d) -> p c d", c=16)
        o += 1024
        o_qf = OFF_B
        o_vt = OFF_B + 2048 + 3072 + 2048
        INN = [arb(o, 256), arb(o + 256, 256), arb(o_qf, 256), arb(o_qf + 512, 256)]
        bINN = [Buf(), Buf(), bQF[0], bQF[1]]
        o += 512
        SQ2 = [arf(o, 512), arf(o + 512, 512), arf(o_vt, 512), arf(o_vt + 512, 512)]
        bSQ2 = [Buf(), Buf(), bvT[0], bvT[1]]
        bST = [Buf(), Buf(), Buf(), Buf()]
        o += 1024
        YN = [arf(o, 512), arf(o + 512, 512), arf(o_qf + 1024, 512), arf(o_qf + 1536, 512)]
        bYN = [Buf(), Buf(), bQS[0], bQS[1]]
        o += 1024
        yT = arb(o, 1024)
        byT = Buf()
        o += 1024
        assert o <= 24576
        bq, bqx, bk, bkz, bv, bsz, bsf, bsb = [Buf() for _ in range(8)]
        rcnt = [0]

        def rotary(ps, bps, tt, is_q, h):
            i = rcnt[0] % 2
            rcnt[0] += 1
            sl = slice(tt * 512, (tt + 1) * 512)
            P.act(lambda e: e.copy(out=QF[i], in_=ps), reads=[bps], writes=[bQF[i]])
            P.pool(lambda e: e.tensor_copy(out=QS[i][0:64, :], in_=QF[i][64:128, :]), reads=[bQF[i]], writes=[bQS[i]])
            P.pool(lambda e: e.tensor_copy(out=QS[i][64:128, :], in_=QF[i][0:64, :]), reads=[bQF[i]], writes=[bQS[i]])
            P.dve(lambda e: e.tensor_tensor(out=QF[i], in0=QF[i], in1=COS[:, sl], op=ALU.mult),
                  reads=[bQF[i], bC['cos'], bQS[i]], writes=[bQF[i]])
            P.pool(lambda e: e.tensor_tensor(out=QS[i], in0=QS[i], in1=SIN[:, sl], op=ALU.mult),
                   reads=[bQS[i], bC['sin']], writes=[bQS[i]])
            P.dve(lambda e: e.tensor_tensor(out=QF[i], in0=QF[i], in1=QS[i], op=ALU.add),
                  reads=[bQF[i], bQS[i]], writes=[bQF[i]])
            if is_q:
                P.act(lambda e: e.copy(out=qT[:, sl], in_=QF[i]), reads=[bQF[i]], writes=[bq])
                P.pool(lambda e: e.tensor_tensor(out=qxT[:, sl].rearrange("p (c q) -> p c q", c=4),
                                                 in0=QF[i].rearrange("p (c q) -> p c q", c=4),
                                                 in1=XI[:, h, :].unsqueeze(1).to_broadcast([128, 4, 128]), op=ALU.mult),
                       reads=[bQF[i], bC['xi']], writes=[bqx])
            else:
                P.act(lambda e: e.copy(out=kT[:, sl], in_=QF[i]), reads=[bQF[i]], writes=[bk])
                pb = 4 + (tt % 2)
                for j in range(4):
                    P.pe(lambda e, j=j: e.transpose(PS[pb][:, j * 128:(j + 1) * 128], QF[i][:, j * 128:(j + 1) * 128], IDN[:]),
                         reads=[bQF[i], bC['idn']], writes=[bPS[pb]])
                P.dve(lambda e: e.tensor_scalar(out=KZ[:, tt * 4:(tt + 1) * 4, :].rearrange("p c d -> p (c d)"),
                                                in0=PS[pb][:, :], scalar1=ZETA[:, h:h + 1], scalar2=None, op0=ALU.mult),
                      reads=[bPS[pb], bC['zeta']], writes=[bkz])

        def ret_load(h):
            return load_w4(l, [RQ + h * 128, RK + h * 128, RV + h * 128, RZ + h * 128])

        def ret_head(h, wi, wv):

            def ev_q(tt, ps, bps):
                rotary(ps, bps, tt, True, h)

            def ev_k(tt, ps, bps):
                rotary(ps, bps, tt, False, h)

            def ev_v(tt, ps, bps):
                i = tt % 2
                evac_copy(tt, vT[i], ps, [bps], [bvT[i]])
                pb = 6 + i
                for j in range(4):
                    P.pe(lambda e, j=j: e.transpose(PS[pb][:, j * 128:(j + 1) * 128], vT[i][:, j * 128:(j + 1) * 128], IDN[:]),
                         reads=[bvT[i], bC['idn']], writes=[bPS[pb]])
                evac_copy(tt + 1, V[:, tt * 4:(tt + 1) * 4, :].rearrange("p c d -> p (c d)"), PS[pb][:, :], [bPS[pb]], [bv])

            def ev_z(tt, ps, bps):
                P.act(lambda e: e.activation(out=SZ[:, tt * 512:(tt + 1) * 512], in_=ps, func=AF.Silu),
                      reads=[bps], writes=[bsz])

            proj(wi, wv, 0, ev_q)
            proj(wi, wv, 1, ev_k)
            proj(wi, wv, 2, ev_v)
            proj(wi, wv, 3, ev_z)

            P.pool(lambda e: e.memset(SF[:, 0, :], 0.0), writes=[bsf])
            def kv_group(g4):
                pb = 6 + g4 % 2
                for j in range(4):
                    c = g4 * 4 + j
                    if c == 15:
                        continue
                    P.pe(lambda e, j=j, c=c: e.matmul(PS[pb][:, j * 128:(j + 1) * 128], lhsT=KZ[:, c, :], rhs=V[:, c, :],
                                                      start=True, stop=True),
                         reads=[bkz, bv], writes=[bPS[pb]])
                for j in range(4):
                    c = g4 * 4 + j
                    if c == 15:
                        continue
                    P.dve(lambda e, j=j, c=c: e.scalar_tensor_tensor(out=SF[:, c + 1, :], in0=SF[:, c, :], scalar=GCH[h],
                                                                     in1=PS[pb][:, j * 128:(j + 1) * 128],
                                                                     op0=ALU.mult, op1=ALU.add),
                          reads=[bsf, bPS[pb]], writes=[bsf])
            for g4 in range(4):
                kv_group(g4)
            P.act(lambda e: e.copy(out=SB[:].rearrange("p c d -> p (c d)"), in_=SF[:].rearrange("p c d -> p (c d)")),
                  reads=[bsf], writes=[bsb])

            def out_group(g4):
                ib = g4
                ii = g4
                for j in range(4):
                    c = g4 * 4 + j
                    P.pe(lambda e, j=j, c=c: e.matmul(PS[ib][:, j * 128:(j + 1) * 128], lhsT=kT[:, c * 128:(c + 1) * 128],
                                                      rhs=qT[:, c * 128:(c + 1) * 128], start=True, stop=True),
                         reads=[bk, bq], writes=[bPS[ib]])
                P.dve(lambda e: e.tensor_tensor(out=INN[ii].rearrange("p (c q) -> p c q", c=4),
                                                in0=PS[ib][:, :].rearrange("p (c q) -> p c q", c=4),
                                                in1=DEC[:, h, :].unsqueeze(1).to_broadcast([128, 4, 128]), op=ALU.mult),
                      reads=[bPS[ib], bC['decay']], writes=[bINN[ii]])
                yield
                ob = 4 + g4
                for j in range(4):
                    c = g4 * 4 + j
                    P.pe(lambda e, j=j, c=c: e.matmul(PS[ob][:, j * 128:(j + 1) * 128], lhsT=INN[ii][:, j * 128:(j + 1) * 128],
                                                      rhs=V[:, c, :], start=True, stop=(c == 0)),
                         reads=[bINN[ii], bv], writes=[bPS[ob]])
                    if c > 0:
                        P.pe(lambda e, j=j, c=c: e.matmul(PS[ob][:, j * 128:(j + 1) * 128], lhsT=qxT[:, c * 128:(c + 1) * 128],
                                                          rhs=SB[:, c, :], start=False, stop=True),
                             reads=[bqx, bsb], writes=[bPS[ob]])
                yield
                so = 16 + 16 * g4
                s1 = STAT[:, so:so + 4]
                s2 = STAT[:, so + 4:so + 8]
                s3 = STAT[:, so + 8:so + 12]
                s4 = STAT[:, so + 12:so + 16]
                o3 = PS[ob][:, :].rearrange("p (c d) -> p c d", c=4)
                P.dve(lambda e: e.reduce_sum(out=s1, in_=o3, axis=AX.X), reads=[bPS[ob]], writes=[bST[ii]])
                P.act(lambda e: e.activation(out=SQ2[ii], in_=PS[ob][:, :], func=AF.Square), reads=[bPS[ob]], writes=[bSQ2[ii]])
                yield
                P.dve(lambda e: e.reduce_sum(out=s2, in_=SQ2[ii].rearrange("p (c d) -> p c d", c=4), axis=AX.X),
                      reads=[bSQ2[ii]], writes=[bST[ii]])
                P.dve(lambda e: e.tensor_scalar(out=s1, in0=s1, scalar1=1.0 / 128, scalar2=None, op0=ALU.mult),
                      reads=[bST[ii]], writes=[bST[ii]])
                P.dve(lambda e: e.tensor_tensor(out=s3, in0=s1, in1=s1, op=ALU.mult),
                      reads=[bST[ii]], writes=[bST[ii]])
                P.dve(lambda e: e.scalar_tensor_tensor(out=s2, in0=s2, scalar=1.0 / 128, in1=s3, op0=ALU.mult, op1=ALU.subtract),
                      reads=[bST[ii]], writes=[bST[ii]])
                P.dve(lambda e: e.tensor_scalar(out=s2, in0=s2, scalar1=EPS, scalar2=None, op0=ALU.add),
                      reads=[bST[ii]], writes=[bST[ii]])
                yield
                P.act(lambda e: e.activation(out=s2, in_=s2, func=AF.Ln), reads=[bST[ii]], writes=[bST[ii]])
                P.act(lambda e: e.activation(out=s2, in_=s2, func=AF.Exp, scale=-0.5), reads=[bST[ii]], writes=[bST[ii]])
                yield
                P.dve(lambda e: e.scalar_tensor_tensor(out=s4, in0=s1, scalar=-1.0, in1=s2, op0=ALU.mult, op1=ALU.mult),
                      reads=[bST[ii]], writes=[bST[ii]])
                for j in range(4):
                    P.act(lambda e, j=j: e.activation(out=YN[ii][:, j * 128:(j + 1) * 128], in_=PS[ob][:, j * 128:(j + 1) * 128],
                                                      func=AF.Identity, bias=s4[:, j:j + 1], scale=s2[:, j:j + 1]),
                          reads=[bPS[ob], bST[ii]], writes=[bYN[ii]])
                yield
                for j in range(4):
                    P.pe(lambda e, j=j: e.transpose(PS[ib][:, j * 128:(j + 1) * 128], YN[ii][:, j * 128:(j + 1) * 128], IDN[:]),
                         reads=[bYN[ii], bC['idn']], writes=[bPS[ib]])
                sl = slice(g4 * 512, (g4 + 1) * 512)
                P.dve(lambda e, sl=sl: e.scalar_tensor_tensor(out=yT[:, sl], in0=PS[ib][:, :], scalar=GNG[:, l, h:h + 1],
                                                              in1=SZ[:, sl], op0=ALU.mult, op1=ALU.mult),
                      reads=[bPS[ib], bC['gng'], bsz], writes=[byT])
            for pair in ((0, 1, 2, 3),):
                gens = [out_group(g) for g in pair]
                while gens:
                    for g_ in list(gens):
                        try:
                            next(g_)
                        except StopIteration:
                            gens.remove(g_)
            store_y(yT, byT, h)

        nxt = ret_load(0)
        for h in range(NH):
            cur = nxt
            if h + 1 < NH:
                nxt = ret_load(h + 1)
            ret_head(h, cur[0], cur[1])

    def phase_pool(l):
        o = OFF_B
        U = [arf(o, 2048), arf(o + 2048, 2048)]
        o += 4096
        SA = arf(o, 2048)
        SBb = arf(o + 2048, 2048)
        o += 4096
        SZ = [arf(o, 2048), arf(o + 2048, 2048)]
        o += 4096
        PT = [arb(o, 1024), arb(o + 1024, 1024)]
        o += 2048
        yT = arb(o, 1024)
        o += 1024
        T16 = arf(o, 16)
        o += 16
        assert o <= 24576
        bU, bSZ, bPT = [Buf(), Buf()], [Buf(), Buf()], [Buf(), Buf()]
        bSA, bSB, byT, bT16 = Buf(), Buf(), Buf(), Buf()
        pstg = Stager(0, 3)

        def pool_load(g):
            a, b_ = 2 * g, 2 * g + 1
            return load_w4s(pstg, l, [PU + a * 128, PU + b_ * 128, PZ + a * 128, PZ + b_ * 128])

        def do_group(g, wi, wv):
            w = POOL_W[g]
            a, b_ = 2 * g, 2 * g + 1
            src = pool_w[l, g].rearrange("(c p) d -> p c d", p=128)
            P.dma('pool', PW[:], src, writes=[bC['pw']])
            for ci in range(2):
                def ev_u(tt, ps, bps, ci=ci):
                    evac_copy(tt, U[ci][:, tt * 512:(tt + 1) * 512], ps, [bps], [bU[ci]])

                def ev_z(tt, ps, bps, ci=ci):
                    P.act(lambda e: e.activation(out=SZ[ci][:, tt * 512:(tt + 1) * 512], in_=ps, func=AF.Silu),
                          reads=[bps], writes=[bSZ[ci]])
                proj(wi, wv, ci, ev_u)
                proj(wi, wv, 2 + ci, ev_z)
            for ci in range(2):
                cur, bcur = U[ci], bU[ci]
                nxt = [(SA, bSA), (SBb, bSB)]
                step = 1
                k = 0
                while step < w:
                    dst, bdst = nxt[k % 2]
                    eng = P.pool if k % 2 == 0 else P.dve
                    eng(lambda e, dst=dst, cur=cur, step=step: e.tensor_tensor(out=dst[:, step:S], in0=cur[:, step:S],
                                                                                in1=cur[:, 0:S - step], op=ALU.add),
                        reads=[bcur], writes=[bdst])
                    eng(lambda e, dst=dst, cur=cur, step=step: e.tensor_copy(out=dst[:, 0:step], in_=cur[:, 0:step]),
                        reads=[bcur], writes=[bdst])
                    cur, bcur = dst, bdst
                    step *= 2
                    k += 1
                P.dve(lambda e, cur=cur, ci=ci: e.scalar_tensor_tensor(out=PT[ci], in0=cur, scalar=1.0 / w, in1=U[ci],
                                                                       op0=ALU.mult, op1=ALU.subtract),
                      reads=[bcur, bU[ci]], writes=[bPT[ci]])
                P.dve(lambda e, cur=cur: e.tensor_tensor(out=T16, in0=cur[:, 0:16], in1=R16[:, g, :], op=ALU.mult),
                      reads=[bcur, bC['r16']], writes=[bT16])
                P.dve(lambda e, ci=ci: e.tensor_tensor(out=PT[ci][:, 0:16], in0=T16, in1=U[ci][:, 0:16], op=ALU.subtract),
                      reads=[bT16, bU[ci], bPT[ci]], writes=[bPT[ci]])
            for dd in range(2):
                for tt in range(4):
                    pb = 4 + tt
                    for cc in range(2):
                        P.pe(lambda e, cc=cc, tt=tt, dd=dd, pb=pb: e.matmul(
                            PS[pb][:, :], lhsT=PW[:, cc, dd * 128:(dd + 1) * 128], rhs=PT[cc][:, tt * 512:(tt + 1) * 512],
                            start=(cc == 0), stop=(cc == 1)),
                            reads=[bC['pw'], bPT[cc]], writes=[bPS[pb]])
                    sl = slice(tt * 512, (tt + 1) * 512)
                    P.dve(lambda e, sl=sl, dd=dd, pb=pb: e.scalar_tensor_tensor(
                        out=yT[:, sl], in0=PS[pb][:, :], scalar=PSC[:, l, 2 * g + dd:2 * g + dd + 1], in1=SZ[dd][:, sl],
                        op0=ALU.mult, op1=ALU.mult),
                        reads=[bPS[pb], bC['psc'], bSZ[dd]], writes=[byT])
                store_y(yT, byT, 16 + 2 * g + dd)
        nxt = pool_load(0)
        for g in range(4):
            cur = nxt
            if g + 1 < 4:
                nxt = pool_load(g + 1)
            do_group(g, cur[0], cur[1])

    def phase_c(l):
        YH = arb(0, 12288).rearrange("p (k t) -> p k t", k=24)
        bYH = [Buf() for _ in range(24)]
        o = 12288
        SG = [[arf(o + (i * 3 + j) * 512, 512) for j in range(3)] for i in range(2)]
        bSG = [[Buf() for j in range(3)] for i in range(2)]
        o += 3072
        ACC = [arf(o, 512), arf(o + 512, 512)]
        TMP = [arf(o + 1024, 512), arf(o + 1536, 512)]
        bACC, bTMP = [Buf(), Buf()], [Buf(), Buf()]
        o += 2048
        MT = [arb(o, 512), arb(o + 512, 512)]
        bMT = [Buf(), Buf()]
        o += 1024
        assert o <= 24576
        branch_w = (w_rb, w_fb, w_pb)
        itc = [0]

        cstg = Stager(18432, 3)

        def c_load(dc):
                i = wb_next()
                gv = WB[i][:, 0:6144].rearrange("p (k c) -> p k c", k=KC)
                bv_ = WB[i][:, 6144:9216].rearrange("p (k c) -> p k c", k=24)
                for j, c0 in enumerate((GA, GB, GC)):
                    src = w_in[l, :, c0 + dc * 128:c0 + (dc + 1) * 128].rearrange("(k p) c -> p k c", p=128)
                    cstg.load(gv[:, :, j * 128:(j + 1) * 128], src, KC, [bWB[i]])
                for j in range(3):
                    src = branch_w[j][l, :, dc * 128:(dc + 1) * 128].rearrange("(k p) c -> p k c", p=128)
                    cstg.load(bv_[:, j * 8:(j + 1) * 8, :], src, 8, [bWB[i]])
                return i, gv, bv_

        def do_dc(half, dc, t0, i, gv, bv_):
                mi = dc % 2
                for t2 in range(2):
                    tt = half * 2 + t2
                    si = itc[0] % 2
                    itc[0] += 1
                    gb0 = 0 if si == 0 else 3
                    for j in range(3):
                        pb = gb0 + j
                        for k in range(KC):
                            P.pe(lambda e, k=k, j=j, pb=pb, tt=tt: e.matmul(
                                PS[pb][:, :], lhsT=gv[:, k, j * 128:(j + 1) * 128], rhs=HT[:, k, tt * 512:(tt + 1) * 512],
                                start=(k == 0), stop=(k == KC - 1)),
                                reads=[bWB[i]] + bHT[tt], writes=[bPS[pb]])
                        P.act(lambda e, j=j, pb=pb, si=si: e.activation(out=SG[si][j], in_=PS[pb][:, :], func=AF.Sigmoid),
                              reads=[bPS[pb]], writes=[bSG[si][j]])
                    for j in range(3):
                        pb = 6 + (j % 2)
                        for k in range(8):
                            P.pe(lambda e, k=k, j=j, pb=pb, t2=t2: e.matmul(
                                PS[pb][:, :], lhsT=bv_[:, j * 8 + k, :], rhs=YH[:, j * 8 + k, t2 * 512:(t2 + 1) * 512],
                                start=(k == 0), stop=(k == 7)),
                                reads=[bWB[i], bYH[j * 8 + k]], writes=[bPS[pb]])
                        if j == 0:
                            P.dve(lambda e, pb=pb, si=si: e.tensor_tensor(out=ACC[si], in0=PS[pb][:, :], in1=SG[si][0], op=ALU.mult),
                                  reads=[bPS[pb], bSG[si][0]], writes=[bACC[si]])
                        else:
                            P.dve(lambda e, pb=pb, si=si, j=j: e.tensor_tensor(out=TMP[si], in0=PS[pb][:, :], in1=SG[si][j], op=ALU.mult),
                                  reads=[bPS[pb], bSG[si][j]], writes=[bTMP[si]])
                            if j == 1:
                                P.dve(lambda e, si=si: e.tensor_tensor(out=ACC[si], in0=ACC[si], in1=TMP[si], op=ALU.add),
                                      reads=[bACC[si], bTMP[si]], writes=[bACC[si]])
                            else:
                                P.dve(lambda e, si=si, t2=t2, mi=mi: e.tensor_tensor(out=MT[mi][:, t2 * 512:(t2 + 1) * 512],
                                                                                     in0=ACC[si], in1=TMP[si], op=ALU.add),
                                      reads=[bACC[si], bTMP[si]], writes=[bMT[mi]])
                P.dma('sp', Md[dc * 128:(dc + 1) * 128, t0:t0 + 1024], MT[mi], reads=[bMT[mi]], writes=[bMd[dc][half]])

        cnxt = [c_load(0)]
        for half in range(2):
            t0 = half * 1024
            for k in range(24):
                P.dma('sp', YH[:, k, :], Yd[k * 128:(k + 1) * 128, t0:t0 + 1024], reads=[bYd[k]], writes=[bYH[k]])
            for dc in range(16):
                cur = cnxt[0]
                nd = half * 16 + dc + 1
                if nd < 32:
                    cnxt[0] = c_load(nd % 16)
                do_dc(half, dc, t0, cur[0], cur[1], cur[2])

    def phase_d(l, xsrc, bxsrc, xdst, bxdst, final):
        MTT = [arb(0, 4096).rearrange("p (k t) -> p k t", k=KC), arb(4096, 4096).rearrange("p (k t) -> p k t", k=KC)]
        bMTT = [Buf(), Buf()]
        XI_ = [arf(8192, 2048), arf(10240, 2048)]
        XO = [arf(12288, 2048), arf(14336, 2048)]
        FG = arf(16384, 2048)
        SQ = arf(18432, 2048)
        bXI, bXO = [Buf(), Buf()], [Buf(), Buf()]
        bFG, bSQ = Buf(), Buf()
        dstg = Stager(20480, 2)
        for c16 in range(16):
            src = w_out[l, :, c16 * 128:(c16 + 1) * 128].rearrange("(k p) c -> p k c", p=128)
            dstg.load(HT[:, :, c16 * 128:(c16 + 1) * 128], src, KC, bHT[c16 // 4])
        if final:
            P.dma('sp', FG, final_g.broadcast_to([128, D]), writes=[bFG])
        for tt in range(4):
            mi = tt % 2
            for dc in range(16):
                P.dma('sp', MTT[mi][:, dc, :], Md[dc * 128:(dc + 1) * 128, tt * 512:(tt + 1) * 512],
                      reads=[bMd[dc][tt // 2]], writes=[bMTT[mi]])
            for t4 in range(4):
                t = tt * 4 + t4
                xi_ = t % 2
                P.dma('act', XI_[xi_], xsrc[t * 128:(t + 1) * 128, :], reads=([bxsrc[t]] if bxsrc else []), writes=[bXI[xi_]])
                pbase = 0 if t % 2 == 0 else 4
                for cb in range(4):
                    for k in range(KC):
                        P.pe(lambda e, k=k, cb=cb, t4=t4, mi=mi, pbase=pbase: e.matmul(
                            PS[pbase + cb][:, :], lhsT=MTT[mi][:, k, t4 * 128:(t4 + 1) * 128],
                            rhs=HT[:, k, cb * 512:(cb + 1) * 512], start=(k == 0), stop=(k == KC - 1)),
                            reads=[bMTT[mi]] + bHT[cb], writes=[bPS[pbase + cb]])
                for cb in range(4):
                    sl = slice(cb * 512, (cb + 1) * 512)
                    P.dve(lambda e, cb=cb, sl=sl, xi_=xi_, pbase=pbase: e.tensor_tensor(
                        out=XO[xi_][:, sl], in0=PS[pbase + cb][:, :], in1=XI_[xi_][:, sl], op=ALU.add),
                        reads=[bPS[pbase + cb], bXI[xi_]], writes=[bXO[xi_]])
                if final:
                    rstd = rms_stats(XO[xi_], bXO[xi_], SQ, bSQ, 8 + 4 * xi_)
                    P.dve(lambda e, xi_=xi_, rstd=rstd: e.scalar_tensor_tensor(out=XO[xi_], in0=XO[xi_], scalar=rstd, in1=FG,
                                                                               op0=ALU.mult, op1=ALU.mult),
                          reads=[bXO[xi_], bC['stat'], bFG], writes=[bXO[xi_]])
                fin.append(P.dma('sp', xdst[t * 128:(t + 1) * 128, :], XO[xi_], reads=[bXO[xi_]],
                                 writes=([bxdst[t]] if bxdst else [])))

    fin = []
    for l in range(n_layers):
        last = (l == n_layers - 1)
        xsrc, bxsrc = (x_in, None) if l == 0 else (x1, bX1)
        xdst, bxdst = (out, None) if last else (x1, bX1)
        phase_a(l, xsrc, bxsrc)
        P.barrier()
        if upto == 'a':
            break
        if upto != 'nofox':
            phase_fox(l)
        P.barrier()
        if upto == 'fox':
            break
        P.dma('sp', COS, ccos, writes=[bC['cos']])
        P.dma('sp', SIN, csin, writes=[bC['sin']])
        P.dma('sp', DEC, cdecay, writes=[bC['decay']])
        P.dma('sp', XI, cxi, writes=[bC['xi']])
        phase_ret(l)
        P.barrier()
        if upto in ('ret', 'nofox'):
            break
        phase_pool(l)
        P.barrier()
        if upto == 'pool':
            break
        phase_c(l)
        P.barrier()
        if upto == 'c':
            break
        del fin[:]
        phase_d(l, xsrc, bxsrc, xdst, bxdst, final=(last and not debug))
        P.barrier()
    fw = list(fin)
    P.emit(final_waits=fw)
    return nc, consts


_CACHE = {}


def _prep_inputs(inputs):
    f = lambda a: np.ascontiguousarray(np.asarray(a, dtype=np.float32))
    shared = {
        'norm_g': f(np.asarray(inputs['norm_g']).reshape(2, KC, 128).transpose(0, 2, 1)),
        'w_in': f(inputs['w_in']),
        'ret_gn_g': f(np.asarray(inputs['ret_gn_g']).reshape(2, NH, 128).transpose(0, 2, 1)),
        'fox_b_f': f(np.asarray(inputs['fox_b_f']).reshape(2, NH).T),
        'pool_w': f(inputs['pool_w']),
        'pool_scale': f(np.asarray(inputs['pool_scale']).reshape(2, 8, 128).transpose(0, 2, 1)),
        'w_ret_branch': f(inputs['w_ret_branch']),
        'w_fox_branch': f(inputs['w_fox_branch']),
        'w_pool_branch': f(inputs['w_pool_branch']),
        'w_out': f(inputs['w_out']),
        'final_g': f(np.asarray(inputs['final_g']).reshape(1, D)),
    }
    return shared


def kernel(**inputs):
    if 'nc' not in _CACHE:
        _CACHE['nc'] = build(2, False)
    nc, consts = _CACHE['nc']
    shared = _prep_inputs(inputs)
    shared.update(consts)
    x = np.asarray(inputs['x'], dtype=np.float32)
    in_maps = []
    for b in range(8):
        m = dict(shared)
        m['x'] = np.ascontiguousarray(x[b])
        in_maps.append(m)
    res = run_bass_kernel_spmd(nc, in_maps, core_ids=list(range(8)))
    return np.stack([np.asarray(r['out'], dtype=np.float32) for r in res.results], axis=0)
```

```python
import numpy as np
import ml_dtypes
import concourse.bass as bass
import concourse.mybir as mybir
from concourse.bass_utils import run_bass_kernel_spmd

F32 = mybir.dt.float32
BF16 = mybir.dt.bfloat16
AF = mybir.ActivationFunctionType
ALU = mybir.AluOpType
AX = mybir.AxisListType

S = 2048
D = 2048
KC = 16
NH = 8
C_IN = 16392
RQ, RK, RV, RZ, FQ, FK, FV, FZ, FF, PU, PZ, GA, GB, GC = (
    0, 1024, 2048, 3072, 4096, 5120, 6144, 7168, 8192, 8200, 9224, 10248, 12296, 14344)
EPS = 1e-6
POOL_W = (2, 4, 8, 16)


class Buf:
    __slots__ = ('w', 'rc', 'rd', 'name', 'excl')

    def __init__(self, name='', excl=False):
        self.w = None
        self.rc = {}
        self.rd = []
        self.name = name
        self.excl = excl


class Op:
    __slots__ = ('stream', 'idx', 'fn', 'waits', 'dma', 'sem', 'semval', 'marked', 'cnt')

    def __init__(self, stream, idx, fn, dma):
        self.stream = stream
        self.idx = idx
        self.fn = fn
        self.dma = dma
        self.waits = []
        self.sem = None
        self.semval = 0
        self.marked = False
        self.cnt = 0


class Prog:
    STREAMS = ('pe', 'act', 'dve', 'pool', 'sp')

    def __init__(self, nc, ndma_sems=None):
        self.nc = nc
        self.ops = {s: [] for s in self.STREAMS}
        self.waited = {s: {p: -1 for p in self.STREAMS} for s in self.STREAMS}
        self.waited_dma = {s: {} for s in self.STREAMS}
        self.csem = {s: nc.alloc_semaphore(name='c_' + s) for s in ('pe', 'act', 'dve', 'pool')}
        nd = ndma_sems or {'sp': 12, 'act': 6, 'pool': 10}
        self.dsems = {s: [nc.alloc_semaphore(name='d_%s%d' % (s, i)) for i in range(n)] for s, n in nd.items()}
        self.dcount = {s: 0 for s in nd}
        self.dhist = {s: [] for s in nd}
        self.pending = {s: [] for s in self.STREAMS}

    def _wait_on(self, o, p, kind):
        if p is o:
            return
        if p.dma:
            key = id(p.sem)
            if self.waited_dma[o.stream].get(key, 0) >= p.semval:
                return
            self.waited_dma[o.stream][key] = p.semval
            o.waits.append(p)
            return
        if p.stream == o.stream and not o.dma:
            if p.stream == 'pe':
                return
        if self.waited[o.stream][p.stream] >= p.idx:
            return
        self.waited[o.stream][p.stream] = p.idx
        p.marked = True
        o.waits.append(p)

    def op(self, stream, fn, reads=(), writes=(), dma=False):
        o = Op(stream, len(self.ops[stream]), fn, dma)
        ex = [b for b in reads if b.excl]
        if ex:
            writes = list(writes) + ex
            reads = [b for b in reads if not b.excl]
        if self.pending[stream]:
            for p in self.pending[stream]:
                self._wait_on(o, p, 'raw')
            self.pending[stream] = []
        if dma:
            k = self.dcount[stream]
            sems = self.dsems[stream]
            n = len(sems)
            o.sem = sems[k % n]
            o.semval = 16 * (k // n + 1)
            if k >= n:
                self._wait_on(o, self.dhist[stream][k - n], 'sem')
            self.dcount[stream] = k + 1
            self.dhist[stream].append(o)
        for b in reads:
            if b.w is not None:
                self._wait_on(o, b.w, 'raw')
        for b in writes:
            if b.w is not None:
                self._wait_on(o, b.w, 'waw')
            for r in b.rc.values():
                self._wait_on(o, r, 'war')
            for r in b.rd:
                self._wait_on(o, r, 'war')
        for b in reads:
            if dma:
                b.rd.append(o)
            else:
                b.rc[stream] = o
        for b in writes:
            b.w = o
            b.rc = {}
            b.rd = []
        self.ops[stream].append(o)
        return o

    def barrier(self):
        tails = []
        for s in self.STREAMS:
            lst = self.ops[s]
            for o in reversed(lst):
                if not o.dma:
                    tails.append(o)
                    break
        for s in self.dhist:
            h = self.dhist[s]
            n = len(self.dsems[s])
            tails.extend(h[-n:])
        for s in self.STREAMS:
            self.pending[s] = list(tails)

    def pe(self, fn, reads=(), writes=()):
        return self.op('pe', fn, reads, writes)

    def act(self, fn, reads=(), writes=()):
        return self.op('act', fn, reads, writes)

    def dve(self, fn, reads=(), writes=()):
        return self.op('dve', fn, reads, writes)

    def pool(self, fn, reads=(), writes=()):
        return self.op('pool', fn, reads, writes)

    def dma(self, stream, out, in_, reads=(), writes=()):
        return self.op(stream, lambda e: e.dma_start(out=out, in_=in_), reads, writes, dma=True)

    def emit(self, final_waits=()):
        nc = self.nc
        for s in self.STREAMS:
            c = 0
            for o in self.ops[s]:
                if o.marked and not o.dma:
                    c += 1
                o.cnt = c
        csem = self.csem

        def run(s, eng, extra=()):
            for o in self.ops[s]:
                for p in o.waits:
                    if p.dma:
                        eng.wait_ge(p.sem, p.semval)
                    else:
                        eng.wait_ge(csem[p.stream], p.cnt)
                ins = o.fn(eng)
                if o.dma:
                    ins.then_inc(o.sem, 16)
                elif o.marked:
                    ins.then_inc(csem[s], 1)
            for p in extra:
                eng.wait_ge(p.sem, p.semval)

        with nc.Block() as block:
            @block.tensor
            def _(eng):
                run('pe', eng)

            @block.scalar
            def _(eng):
                run('act', eng)

            @block.vector
            def _(eng):
                run('dve', eng)

            @block.gpsimd
            def _(eng):
                run('pool', eng)

            @block.sync
            def _(eng):
                run('sp', eng, final_waits)


def make_consts():
    c = {}
    c['cidn'] = np.eye(128, dtype=np.float32)
    p = np.arange(128)
    negm = np.where(p[:, None] > p[None, :], -30000.0, 0.0).astype(np.float32)
    cb = np.concatenate([np.eye(128, dtype=np.float32), np.ones((128, 128), np.float32), negm], axis=1)
    c['cbf'] = cb.astype(ml_dtypes.bfloat16)
    half = 64
    inv = (np.float32(10000.0) ** (-(np.arange(half, dtype=np.float32)) / np.float32(half))).astype(np.float32)
    pos = np.arange(S, dtype=np.float32)
    ang = (pos[None, :] * inv[:, None]).astype(np.float32)
    cs = np.cos(ang).astype(np.float32)
    sn = np.sin(ang).astype(np.float32)
    c['ccos'] = np.concatenate([cs, cs], axis=0)
    c['csin'] = np.concatenate([-sn, sn], axis=0)
    hh = np.arange(NH, dtype=np.float64)
    log_g = np.log1p(-np.exp2(-5.0 - hh))
    scale = 128.0 ** -0.5
    idx = np.arange(128, dtype=np.float64)
    diff = idx[None, :] - idx[:, None]
    dec = np.where(diff[None] >= 0, np.exp(log_g[:, None, None] * np.maximum(diff[None], 0.0)), 0.0) * scale
    c['cdecay'] = np.ascontiguousarray(dec.transpose(1, 0, 2)).astype(np.float32)
    xi = np.exp(log_g[:, None] * (idx[None, :] + 1.0))
    c['cxi'] = np.ascontiguousarray(np.broadcast_to(xi[None], (128, NH, 128))).astype(np.float32)
    zeta = np.exp(log_g[:, None] * (127.0 - idx[None, :])) * scale
    c['czeta'] = np.ascontiguousarray(zeta.T).astype(np.float32)
    r16 = np.zeros((4, 16), np.float64)
    for g, w in enumerate(POOL_W):
        r16[g] = 1.0 / np.minimum(np.arange(16) + 1.0, float(w))
    c['cr16'] = np.ascontiguousarray(np.broadcast_to(r16[None], (128, 4, 16))).astype(np.float32)
    gch = [float(np.exp(log_g[h] * 128.0)) for h in range(NH)]
    return c, gch


def build(n_layers=2, debug=False, upto='d'):
    nc = bass.Bass("TRN2", target_bir_lowering=False)
    consts, GCH = make_consts()
    NL = n_layers

    def din(name, shape, dt=F32):
        if upto == 'a' and name in ('w_in', 'pool_w', 'w_ret_branch', 'w_fox_branch', 'w_pool_branch', 'w_out'):
            shape = [NL, 1, 1]
        return nc.dram_tensor(name, list(shape), dt, kind="ExternalInput").ap()

    x_in = din("x", [S, D])
    norm_g = din("norm_g", [NL, 128, KC])
    w_in = din("w_in", [NL, D, C_IN])
    gn_g = din("ret_gn_g", [NL, 128, NH])
    fox_b = din("fox_b_f", [NH, NL])
    pool_w = din("pool_w", [NL, 4, 256, 256])
    pool_s = din("pool_scale", [NL, 128, 8])
    w_rb = din("w_ret_branch", [NL, 1024, D])
    w_fb = din("w_fox_branch", [NL, 1024, D])
    w_pb = din("w_pool_branch", [NL, 1024, D])
    w_out = din("w_out", [NL, D, D])
    final_g = din("final_g", [1, D])
    cidn = din("cidn", [128, 128])
    cbf = din("cbf", [128, 384], BF16)
    ccos = din("ccos", [128, S])
    csin = din("csin", [128, S])
    cdecay = din("cdecay", [128, NH, 128])
    cxi = din("cxi", [128, NH, 128])
    czeta = din("czeta", [128, NH])
    cr16 = din("cr16", [128, 4, 16])
    out = nc.dram_tensor("out", [S, D], F32, kind="ExternalOutput").ap()
    x1 = nc.dram_tensor("x1s", [S, D], F32).ap()
    Yd = nc.dram_tensor("Yd", [3072, S], BF16, kind=("ExternalOutput" if debug else "Internal")).ap()
    Md = nc.dram_tensor("Md", [D, S], BF16).ap()

    P = Prog(nc)

    HT = nc.alloc_sbuf_tensor("HT", [128, KC, S], BF16)
    WB = [nc.alloc_sbuf_tensor("WB%d" % i, [128, 9216], BF16) for i in range(2)]
    AR = nc.alloc_sbuf_tensor("AR", [128, 24576], F32)
    IDN = nc.alloc_sbuf_tensor("IDN", [128, 128], F32)
    CBF = nc.alloc_sbuf_tensor("CBF", [128, 384], BF16)
    ONESF = nc.alloc_sbuf_tensor("ONESF", [128, 128], F32)
    ZETA = nc.alloc_sbuf_tensor("ZETA", [128, NH], F32)
    R16 = nc.alloc_sbuf_tensor("R16", [128, 4, 16], F32)
    GCOL = nc.alloc_sbuf_tensor("GCOL", [128, NL, KC], F32)
    GNG = nc.alloc_sbuf_tensor("GNG", [128, NL, NH], F32)
    PSC = nc.alloc_sbuf_tensor("PSC", [128, NL, 8], F32)
    FBC = nc.alloc_sbuf_tensor("FBC", [NH, NL], F32)
    WFF = nc.alloc_sbuf_tensor("WFF", [128, KC, NH], BF16)
    PW = nc.alloc_sbuf_tensor("PW", [128, 2, 256], BF16)
    CT = nc.alloc_sbuf_tensor("CT", [128, 16, NH], F32)
    CREF = nc.alloc_sbuf_tensor("CREF", [128, NH, 16], F32)
    CD = nc.alloc_sbuf_tensor("CD", [NH, NH, 16], F32)
    STAT = nc.alloc_sbuf_tensor("STAT", [128, 96], F32)
    PS = [nc.alloc_psum_tensor("PS%d" % i, [128, 512], F32) for i in range(8)]
    bPS = [Buf('ps%d' % i, excl=True) for i in range(8)]
    IDB = CBF[:, 0:128]
    ONEB = CBF[:, 128:256]
    NEGM = CBF[:, 256:384]

    def arf(off, n):
        return AR[:, off:off + n]

    def arb(off, n):
        return AR[:, off:off + n].bitcast(BF16)

    bHT = [[Buf('hta%d' % i), Buf('htd%d' % i)] for i in range(4)]
    bWB = [Buf('wb0'), Buf('wb1')]
    bC = {k: Buf(k) for k in ('idn', 'cbf', 'onesf', 'zeta', 'r16', 'gcol', 'gng', 'psc', 'fbc', 'wff', 'pw',
                              'ct', 'cref', 'cd', 'stat', 'cos', 'sin', 'decay', 'xi')}
    bX1 = [Buf() for _ in range(16)]
    bYd = [Buf() for _ in range(24)]
    bMd = [[Buf() for _ in range(2)] for _ in range(16)]
    wcnt = [0]

    P.dma('sp', IDN[:], cidn, writes=[bC['idn']])
    P.dma('sp', CBF[:], cbf, writes=[bC['cbf']])
    P.dma('sp', ZETA[:], czeta, writes=[bC['zeta']])
    P.dma('sp', R16[:], cr16, writes=[bC['r16']])
    P.dma('sp', GCOL[:], norm_g.rearrange("l p k -> p l k"), writes=[bC['gcol']])
    P.dma('sp', GNG[:], gn_g.rearrange("l p k -> p l k"), writes=[bC['gng']])
    P.dma('sp', PSC[:], pool_s.rearrange("l p k -> p l k"), writes=[bC['psc']])
    P.dma('sp', FBC[:], fox_b, writes=[bC['fbc']])
    P.dve(lambda e: e.memset(ONESF[:], 1.0), writes=[bC['onesf']])

    def evac_copy(i, out_ap, in_ap, reads, writes):
        if i % 2 == 0:
            P.act(lambda e: e.copy(out=out_ap, in_=in_ap), reads, writes)
        else:
            P.dve(lambda e: e.tensor_copy(out=out_ap, in_=in_ap), reads, writes)

    def wb_next():
        i = wcnt[0] % 2
        wcnt[0] += 1
        return i

    def load_w4(l, cols):
        i = wb_next()
        v = WB[i][:, 0:8192].rearrange("p (k c) -> p k c", k=KC)
        for j, c0 in enumerate(cols):
            src = w_in[l, :, c0:c0 + 128].rearrange("(k p) c -> p k c", p=128)
            P.dma('pool', v[:, :, j * 128:(j + 1) * 128], src, writes=[bWB[i]])
        return i, v

    def proj_steps(wi, wv, j, evac):
        for k in range(KC):
            for tt in range(4):
                P.pe(lambda e, k=k, tt=tt: e.matmul(PS[tt][:, :], lhsT=wv[:, k, j * 128:(j + 1) * 128],
                                                    rhs=HT[:, k, tt * 512:(tt + 1) * 512],
                                                    start=(k == 0), stop=(k == KC - 1)),
                     reads=[bWB[wi]] + bHT[tt], writes=[bPS[tt]])
            yield
        for tt in range(4):
            evac(tt, PS[tt][:, :], bPS[tt])
        yield

    class Stager:
        def __init__(self, base, nslots):
            self.slots = [arf(base + i * 2048, 2048) for i in range(nslots)]
            self.bufs = [Buf() for _ in range(nslots)]
            self.cnt = 0

        def load(self, dst, src, K, wbufs):
            s = self.cnt % len(self.slots)
            self.cnt += 1
            st = self.slots[s][:, 0:K * 128].rearrange("p (k c) -> p k c", k=K)
            P.dma('sp', st, src, writes=[self.bufs[s]])
            P.pool(lambda e: e.tensor_copy(out=dst, in_=st), reads=[self.bufs[s]], writes=wbufs)

    def load_w4s(stg, l, cols):
        i = wb_next()
        v = WB[i][:, 0:8192].rearrange("p (k c) -> p k c", k=KC)
        for j, c0 in enumerate(cols):
            src = w_in[l, :, c0:c0 + 128].rearrange("(k p) c -> p k c", p=128)
            stg.load(v[:, :, j * 128:(j + 1) * 128], src, KC, [bWB[i]])
        return i, v

    def proj(wi, wv, j, evac):
        for k in range(KC):
            for tt in range(4):
                P.pe(lambda e, k=k, tt=tt: e.matmul(PS[tt][:, :], lhsT=wv[:, k, j * 128:(j + 1) * 128],
                                                    rhs=HT[:, k, tt * 512:(tt + 1) * 512],
                                                    start=(k == 0), stop=(k == KC - 1)),
                     reads=[bWB[wi]] + bHT[tt], writes=[bPS[tt]])
        for tt in range(4):
            evac(tt, PS[tt][:, :], bPS[tt])

    def rms_stats(xt, bx, sq, bsq, col):
        P.pool(lambda e: e.tensor_tensor(out=sq, in0=xt, in1=xt, op=ALU.mult), reads=[bx], writes=[bsq])
        P.dve(lambda e: e.reduce_sum(out=STAT[:, col:col + 1], in_=sq, axis=AX.X), reads=[bsq], writes=[bC['stat']])
        P.dve(lambda e: e.tensor_scalar(out=STAT[:, col + 1:col + 2], in0=STAT[:, col:col + 1], scalar1=1.0 / D,
                                        scalar2=EPS, op0=ALU.mult, op1=ALU.add),
              reads=[bC['stat']], writes=[bC['stat']])
        P.act(lambda e: e.activation(out=STAT[:, col + 2:col + 3], in_=STAT[:, col + 1:col + 2], func=AF.Ln),
              reads=[bC['stat']], writes=[bC['stat']])
        P.act(lambda e: e.activation(out=STAT[:, col + 2:col + 3], in_=STAT[:, col + 2:col + 3], func=AF.Exp, scale=-0.5),
              reads=[bC['stat']], writes=[bC['stat']])
        return STAT[:, col + 2:col + 3]

    def phase_a(l, xsrc, bxsrc):
        xin = [arf(0, 2048), arf(2048, 2048)]
        xn = [arf(4096, 2048), arf(6144, 2048)]
        sq = arf(8192, 2048)
        bxin = [Buf(), Buf()]
        bxn = [Buf(), Buf()]
        bsq = Buf()
        for t in range(16):
            i = t % 2
            P.dma('sp', xin[i], xsrc[t * 128:(t + 1) * 128, :], reads=([bxsrc[t]] if bxsrc else []), writes=[bxin[i]])
            rstd = rms_stats(xin[i], bxin[i], sq, bsq, 4 * i)
            P.act(lambda e, i=i, rstd=rstd: e.mul(out=xn[i], in_=xin[i], mul=rstd),
                  reads=[bxin[i], bC['stat']], writes=[bxn[i]])
            for g4 in range(4):
                pb = 4 + (g4 % 2) + 2 * (t % 2)
                for j in range(4):
                    k = g4 * 4 + j
                    P.pe(lambda e, i=i, k=k, j=j, pb=pb: e.transpose(PS[pb][:, j * 128:(j + 1) * 128],
                                                                     xn[i][:, k * 128:(k + 1) * 128], IDN[:]),
                         reads=[bxn[i], bC['idn']], writes=[bPS[pb]])
                for j in range(4):
                    k = g4 * 4 + j
                    o_ap = HT[:, k, t * 128:(t + 1) * 128]
                    i_ap = PS[pb][:, j * 128:(j + 1) * 128]
                    g_ap = GCOL[:, l, k:k + 1]
                    if g4 % 2 == 0:
                        P.act(lambda e, o_ap=o_ap, i_ap=i_ap, g_ap=g_ap: e.mul(out=o_ap, in_=i_ap, mul=g_ap),
                              reads=[bPS[pb], bC['gcol']], writes=[bHT[t // 4][0]])
                    else:
                        P.dve(lambda e, o_ap=o_ap, i_ap=i_ap, g_ap=g_ap: e.tensor_scalar(
                            out=o_ap, in0=i_ap, scalar1=g_ap, scalar2=None, op0=ALU.mult),
                            reads=[bPS[pb], bC['gcol']], writes=[bHT[t // 4][1]])

    OFF_COS, OFF_SIN, OFF_DEC, OFF_XI, OFF_B = 0, 2048, 4096, 5120, 6144
    COS = arf(OFF_COS, 2048)
    SIN = arf(OFF_SIN, 2048)
    DEC = arf(OFF_DEC, 1024).rearrange("p (h q) -> p h q", h=NH)
    XI = arf(OFF_XI, 1024).rearrange("p (h q) -> p h q", h=NH)

    def store_y(yT, by, chunk):
        P.dma('sp', Yd[chunk * 128:(chunk + 1) * 128, :], yT, reads=[by], writes=[bYd[chunk]])

    def phase_fox(l):
        fstg = Stager(0, 3)
        o = OFF_B
        sets = []
        for s_ in range(2):
            d = {'qT': arb(o, 1024), 'kT': arb(o + 1024, 1024),
                 'V': arb(o + 2048, 1024).rearrange("p (c d) -> p c d", c=16),
                 'sz': arf(o + 3072, 2048)}
            d['b'] = {k: Buf() for k in ('qT', 'kT', 'V', 'sz')}
            sets.append(d)
            o += 5120
        vT = [arf(o, 512), arf(o + 512, 512)]
        bvT = [Buf(), Buf()]
        o += 1024
        yT = arb(o, 1024)
        byT = Buf()
        o += 1024
        PP = [arb(o, 256), arb(o + 256, 256)]
        bPP = [Buf(), Buf()]
        o += 512
        RS = arf(o, 512)
        TM = arf(o + 512, 512)
        bRS, bTM = Buf(), Buf()
        o += 1024
        BIAS = arf(o, 256).rearrange("p (a b) -> p a b", a=16)
        bBIAS = Buf()
        o += 256
        LG = AR[0:NH, o:o + 2048]
        CC = AR[0:NH, o + 2048:o + 4096]
        bLG, bCC = Buf(), Buf()
        o += 4096
        assert o <= 24576

        src = w_in[l, :, FF:FF + NH].rearrange("(k p) c -> p k c", p=128)
        P.dma('pool', WFF[:], src, writes=[bC['wff']])
        for k in range(KC):
            for tt in range(4):
                P.pe(lambda e, k=k, tt=tt: e.matmul(PS[tt][0:NH, :], lhsT=WFF[:, k, :], rhs=HT[:, k, tt * 512:(tt + 1) * 512],
                                                    start=(k == 0), stop=(k == KC - 1)),
                     reads=[bC['wff']] + bHT[tt], writes=[bPS[tt]])
        for tt in range(4):
            P.act(lambda e, tt=tt: e.activation(out=LG[:, tt * 512:(tt + 1) * 512], in_=PS[tt][0:NH, :], func=AF.Sigmoid,
                                                bias=FBC[:, l:l + 1], scale=1.0),
                  reads=[bPS[tt], bC['fbc']], writes=[bLG])
        P.act(lambda e: e.activation(out=LG, in_=LG, func=AF.Ln), reads=[bLG], writes=[bLG])
        P.dve(lambda e: e.tensor_tensor_scan(out=CC, data0=ONESF[0:NH, 0:1].to_broadcast([NH, S]), data1=LG,
                                             initial=0.0, op0=ALU.mult, op1=ALU.add),
              reads=[bLG, bC['onesf']], writes=[bCC])
        for kb in range(16):
            P.pe(lambda e, kb=kb: e.transpose(PS[4][:, kb * NH:(kb + 1) * NH], CC[:, kb * 128:(kb + 1) * 128], IDN[0:NH, 0:NH]),
                 reads=[bCC, bC['idn']], writes=[bPS[4]])
        P.dve(lambda e: e.tensor_copy(out=CT[:].rearrange("p a b -> p (a b)"), in_=PS[4][:, 0:128]),
              reads=[bPS[4]], writes=[bC['ct']])
        cmid = CC.rearrange("h (q t) -> h q t", t=128)[:, :, 64:65].rearrange("h q o -> h o q")
        P.dve(lambda e: e.tensor_tensor(out=CD[:], in0=cmid.to_broadcast([NH, NH, 16]),
                                        in1=IDN[0:NH, 0:NH].unsqueeze(2).to_broadcast([NH, NH, 16]), op=ALU.mult),
              reads=[bCC, bC['idn']], writes=[bC['cd']])
        P.pe(lambda e: e.matmul(PS[5][:, 0:128], lhsT=ONESF[0:NH, :], rhs=CD[:].rearrange("h a b -> h (a b)"),
                                start=True, stop=True),
             reads=[bC['onesf'], bC['cd']], writes=[bPS[5]])
        P.dve(lambda e: e.tensor_copy(out=CREF[:].rearrange("p a b -> p (a b)"), in_=PS[5][:, 0:128]),
              reads=[bPS[5]], writes=[bC['cref']])

        def fox_proj(h):
            st = sets[h % 2]
            b = st['b']
            wi, wv = load_w4s(fstg, l, [FQ + h * 128, FK + h * 128, FV + h * 128, FZ + h * 128])

            def ev_q(tt, ps, bps):
                evac_copy(tt, st['qT'][:, tt * 512:(tt + 1) * 512], ps, [bps], [b['qT']])

            def ev_k(tt, ps, bps):
                evac_copy(tt + 1, st['kT'][:, tt * 512:(tt + 1) * 512], ps, [bps], [b['kT']])

            def ev_v(tt, ps, bps):
                i = tt % 2
                evac_copy(tt, vT[i], ps, [bps], [bvT[i]])
                pb = tt
                for j in range(4):
                    P.pe(lambda e, j=j: e.transpose(PS[pb][:, j * 128:(j + 1) * 128], vT[i][:, j * 128:(j + 1) * 128], IDN[:]),
                         reads=[bvT[i], bC['idn']], writes=[bPS[pb]])
                evac_copy(tt + 1, st['V'][:, tt * 4:(tt + 1) * 4, :].rearrange("p c d -> p (c d)"), PS[pb][:, :],
                          [bPS[pb]], [b['V']])

            def ev_z(tt, ps, bps):
                P.act(lambda e: e.activation(out=st['sz'][:, tt * 512:(tt + 1) * 512], in_=ps, func=AF.Silu),
                      reads=[bps], writes=[b['sz']])

            for part, ev in enumerate((ev_q, ev_k, ev_v, ev_z)):
                for _ in proj_steps(wi, wv, part, ev):
                    yield

        def pull(gen, n):
            if gen is None:
                return
            for _ in range(n):
                try:
                    next(gen)
                except StopIteration:
                    return

        def fox_attn(h, gen):
            st = sets[h % 2]
            b = st['b']
            P.dve(lambda e: e.tensor_tensor(out=BIAS, in0=CREF[:, h, :].unsqueeze(1).to_broadcast([128, 16, 16]),
                                            in1=CT[:, :, h].unsqueeze(2).to_broadcast([128, 16, 16]), op=ALU.subtract),
                  reads=[bC['cref'], bC['ct']], writes=[bBIAS])
            scale = 128.0 ** -0.5
            cnt = [0]
            def do_qt(qt):
                nkb = 4 * qt + 4
                q0 = qt * 512

                def s_mm(kb):
                    sb = 4 + (cnt[0] + kb) % 2
                    lo = max(0, kb - 4 * qt) * 128
                    diag = kb >= 4 * qt
                    P.pe(lambda e: e.matmul(PS[sb][:, lo:512], lhsT=st['kT'][:, kb * 128:(kb + 1) * 128],
                                            rhs=st['qT'][:, q0 + lo:q0 + 512], start=True, stop=not diag),
                         reads=[b['kT'], b['qT']], writes=[bPS[sb]])
                    if diag:
                        P.pe(lambda e: e.matmul(PS[sb][:, lo:lo + 128], lhsT=IDB, rhs=NEGM, start=False, stop=True),
                             reads=[bC['cbf']], writes=[bPS[sb]])

                def p_exp(kb):
                    sb = 4 + (cnt[0] + kb) % 2
                    pi = (cnt[0] + kb) % 2
                    lo = max(0, kb - 4 * qt)
                    for j in range(lo, 4):
                        qb = qt * 4 + j
                        P.act(lambda e, j=j, qb=qb: e.activation(out=PP[pi][:, j * 128:(j + 1) * 128],
                                                                 in_=PS[sb][:, j * 128:(j + 1) * 128], func=AF.Exp,
                                                                 bias=BIAS[:, kb, qb:qb + 1], scale=scale),
                              reads=[bPS[sb], bBIAS], writes=[bPP[pi]])

                def pv_mm(kb):
                    pi = (cnt[0] + kb) % 2
                    lo = max(0, kb - 4 * qt) * 128
                    P.pe(lambda e: e.matmul(PS[6][:, lo:512], lhsT=st['V'][:, kb, :], rhs=PP[pi][:, lo:512],
                                            start=(kb == 0), stop=(kb == nkb - 1)),
                         reads=[b['V'], bPP[pi]], writes=[bPS[6]])
                    P.pe(lambda e: e.matmul(PS[7][:, lo:512], lhsT=ONEB, rhs=PP[pi][:, lo:512],
                                            start=(kb == 0), stop=(kb == nkb - 1)),
                         reads=[bC['cbf'], bPP[pi]], writes=[bPS[7]])

                s_mm(0)
                for kb in range(nkb):
                    if kb + 1 < nkb:
                        s_mm(kb + 1)
                    p_exp(kb)
                    pull(gen, 2)
                    pv_mm(kb)
                cnt[0] += nkb
                P.dve(lambda e: e.reciprocal(out=RS, in_=PS[7][:, :]), reads=[bPS[7]], writes=[bRS])
                P.dve(lambda e: e.tensor_tensor(out=TM, in0=PS[6][:, :], in1=RS, op=ALU.mult),
                      reads=[bPS[6], bRS], writes=[bTM])
                P.dve(lambda e: e.tensor_tensor(out=yT[:, q0:q0 + 512], in0=TM, in1=st['sz'][:, q0:q0 + 512], op=ALU.mult),
                      reads=[bTM, b['sz']], writes=[byT])
            for qt in range(4):
                do_qt(qt)
            store_y(yT, byT, 8 + h)

        pull(fox_proj(0), 10 ** 6)
        for h in range(NH):
            gen = fox_proj(h + 1) if h + 1 < NH else None
            fox_attn(h, gen)
            pull(gen, 10 ** 6)

    def phase_ret(l):
        o = OFF_B
        QF = [arf(o, 512), arf(o + 512, 512)]
        QS = [arf(o + 1024, 512), arf(o + 1536, 512)]
        bQF, bQS = [Buf(), Buf()], [Buf(), Buf()]
        o += 2048
        qT = arb(o, 1024)
        qxT = arb(o + 1024, 1024)
        kT = arb(o + 2048, 1024)
        o += 3072
        KZ = arb(o, 1024).rearrange("p (c d) -> p c d", c=16)
        V = arb(o + 1024, 1024).rearrange("p (c d) -> p c d", c=16)
        o += 2048
        vT = [arf(o, 512), arf(o + 512, 512)]
        bvT = [Buf(), Buf()]
        o += 1024
        SZ = arf(o, 2048)
        o += 2048
        SF = arf(o, 2048).rearrange("p (c d) -> p c d", c=16)
        o += 2048
        SB = arb(o, 1024).rearrange("p (c d) -> p c d", c=16)
        o += 1024
        o_qf = OFF_B
        o_vt = OFF_B + 2048 + 3072 + 2048
        INN = [arb(o, 256), arb(o + 256, 256), arb(o_qf, 256), arb(o_qf + 512, 256)]
        bINN = [Buf(), Buf(), bQF[0], bQF[1]]
        o += 512
        SQ2 = [arf(o, 512), arf(o + 512, 512), arf(o_vt, 512), arf(o_vt + 512, 512)]
        bSQ2 = [Buf(), Buf(), bvT[0], bvT[1]]
        bST = [Buf(), Buf(), Buf(), Buf()]
        o += 1024
        YN = [arf(o, 512), arf(o + 512, 512), arf(o_qf + 1024, 512), arf(o_qf + 1536, 512)]
        bYN = [Buf(), Buf(), bQS[0], bQS[1]]
        o += 1024
        yT = arb(o, 1024)
        byT = Buf()
        o += 1024
        assert o <= 24576
        bq, bqx, bk, bkz, bv, bsz, bsf, bsb = [Buf() for _ in range(8)]
        rcnt = [0]

        def rotary(ps, bps, tt, is_q, h):
            i = rcnt[0] % 2
            rcnt[0] += 1
            sl = slice(tt * 512, (tt + 1) * 512)
            P.act(lambda e: e.copy(out=QF[i], in_=ps), reads=[bps], writes=[bQF[i]])
            P.dve(lambda e: e.tensor_copy(out=QS[i][0:64, :], in_=ps[64:128, :]), reads=[bps], writes=[bQS[i]])
            P.dve(lambda e: e.tensor_copy(out=QS[i][64:128, :], in_=ps[0:64, :]), reads=[bps], writes=[bQS[i]])
            P.dve(lambda e: e.tensor_tensor(out=QF[i], in0=QF[i], in1=COS[:, sl], op=ALU.mult),
                  reads=[bQF[i], bC['cos'], bQS[i]], writes=[bQF[i]])
            P.pool(lambda e: e.tensor_tensor(out=QS[i], in0=QS[i], in1=SIN[:, sl], op=ALU.mult),
                   reads=[bQS[i], bC['sin']], writes=[bQS[i]])
            P.dve(lambda e: e.tensor_tensor(out=QF[i], in0=QF[i], in1=QS[i], op=ALU.add),
                  reads=[bQF[i], bQS[i]], writes=[bQF[i]])
            if is_q:
                P.act(lambda e: e.copy(out=qT[:, sl], in_=QF[i]), reads=[bQF[i]], writes=[bq])
                P.pool(lambda e: e.tensor_tensor(out=qxT[:, sl].rearrange("p (c q) -> p c q", c=4),
                                                 in0=QF[i].rearrange("p (c q) -> p c q", c=4),
                                                 in1=XI[:, h, :].unsqueeze(1).to_broadcast([128, 4, 128]), op=ALU.mult),
                       reads=[bQF[i], bC['xi']], writes=[bqx])
            else:
                P.act(lambda e: e.copy(out=kT[:, sl], in_=QF[i]), reads=[bQF[i]], writes=[bk])
                pb = 4 + (tt % 2)
                for j in range(4):
                    P.pe(lambda e, j=j: e.transpose(PS[pb][:, j * 128:(j + 1) * 128], QF[i][:, j * 128:(j + 1) * 128], IDN[:]),
                         reads=[bQF[i], bC['idn']], writes=[bPS[pb]])
                P.dve(lambda e: e.tensor_scalar(out=KZ[:, tt * 4:(tt + 1) * 4, :].rearrange("p c d -> p (c d)"),
                                                in0=PS[pb][:, :], scalar1=ZETA[:, h:h + 1], scalar2=None, op0=ALU.mult),
                      reads=[bPS[pb], bC['zeta']], writes=[bkz])

        def ret_load(h):
            return load_w4(l, [RQ + h * 128, RK + h * 128, RV + h * 128, RZ + h * 128])

        def ret_head(h, wi, wv):

            def ev_q(tt, ps, bps):
                rotary(ps, bps, tt, True, h)

            def ev_k(tt, ps, bps):
                rotary(ps, bps, tt, False, h)

            def ev_v(tt, ps, bps):
                i = tt % 2
                evac_copy(tt, vT[i], ps, [bps], [bvT[i]])
                pb = 6 + i
                for j in range(4):
                    P.pe(lambda e, j=j: e.transpose(PS[pb][:, j * 128:(j + 1) * 128], vT[i][:, j * 128:(j + 1) * 128], IDN[:]),
                         reads=[bvT[i], bC['idn']], writes=[bPS[pb]])
                evac_copy(tt + 1, V[:, tt * 4:(tt + 1) * 4, :].rearrange("p c d -> p (c d)"), PS[pb][:, :], [bPS[pb]], [bv])

            def ev_z(tt, ps, bps):
                P.act(lambda e: e.activation(out=SZ[:, tt * 512:(tt + 1) * 512], in_=ps, func=AF.Silu),
                      reads=[bps], writes=[bsz])

            proj(wi, wv, 0, ev_q)
            proj(wi, wv, 1, ev_k)
            proj(wi, wv, 2, ev_v)
            proj(wi, wv, 3, ev_z)

            P.pool(lambda e: e.memset(SF[:, 0, :], 0.0), writes=[bsf])
            def kv_group(g4):
                pb = 6 + g4 % 2
                for j in range(4):
                    c = g4 * 4 + j
                    if c == 15:
                        continue
                    P.pe(lambda e, j=j, c=c: e.matmul(PS[pb][:, j * 128:(j + 1) * 128], lhsT=KZ[:, c, :], rhs=V[:, c, :],
                                                      start=True, stop=True),
                         reads=[bkz, bv], writes=[bPS[pb]])
                for j in range(4):
                    c = g4 * 4 + j
                    if c == 15:
                        continue
                    P.dve(lambda e, j=j, c=c: e.scalar_tensor_tensor(out=SF[:, c + 1, :], in0=SF[:, c, :], scalar=GCH[h],
                                                                     in1=PS[pb][:, j * 128:(j + 1) * 128],
                                                                     op0=ALU.mult, op1=ALU.add),
                          reads=[bsf, bPS[pb]], writes=[bsf])
            for g4 in range(4):
                kv_group(g4)
            P.act(lambda e: e.copy(out=SB[:].rearrange("p c d -> p (c d)"), in_=SF[:].rearrange("p c d -> p (c d)")),
                  reads=[bsf], writes=[bsb])

            def out_group(g4):
                ib = g4
                ii = g4
                for j in range(4):
                    c = g4 * 4 + j
                    P.pe(lambda e, j=j, c=c: e.matmul(PS[ib][:, j * 128:(j + 1) * 128], lhsT=kT[:, c * 128:(c + 1) * 128],
                                                      rhs=qT[:, c * 128:(c + 1) * 128], start=True, stop=True),
                         reads=[bk, bq], writes=[bPS[ib]])
                P.dve(lambda e: e.tensor_tensor(out=INN[ii].rearrange("p (c q) -> p c q", c=4),
                                                in0=PS[ib][:, :].rearrange("p (c q) -> p c q", c=4),
                                                in1=DEC[:, h, :].unsqueeze(1).to_broadcast([128, 4, 128]), op=ALU.mult),
                      reads=[bPS[ib], bC['decay']], writes=[bINN[ii]])
                yield
                ob = 4 + g4
                for j in range(4):
                    c = g4 * 4 + j
                    P.pe(lambda e, j=j, c=c: e.matmul(PS[ob][:, j * 128:(j + 1) * 128], lhsT=INN[ii][:, j * 128:(j + 1) * 128],
                                                      rhs=V[:, c, :], start=True, stop=(c == 0)),
                         reads=[bINN[ii], bv], writes=[bPS[ob]])
                    if c > 0:
                        P.pe(lambda e, j=j, c=c: e.matmul(PS[ob][:, j * 128:(j + 1) * 128], lhsT=qxT[:, c * 128:(c + 1) * 128],
                                                          rhs=SB[:, c, :], start=False, stop=True),
                             reads=[bqx, bsb], writes=[bPS[ob]])
                yield
                so = 16 + 16 * g4
                s1 = STAT[:, so:so + 4]
                s2 = STAT[:, so + 4:so + 8]
                s3 = STAT[:, so + 8:so + 12]
                s4 = STAT[:, so + 12:so + 16]
                o3 = PS[ob][:, :].rearrange("p (c d) -> p c d", c=4)
                P.dve(lambda e: e.reduce_sum(out=s1, in_=o3, axis=AX.X), reads=[bPS[ob]], writes=[bST[ii]])
                P.act(lambda e: e.activation(out=SQ2[ii], in_=PS[ob][:, :], func=AF.Square), reads=[bPS[ob]], writes=[bSQ2[ii]])
                yield
                P.dve(lambda e: e.reduce_sum(out=s2, in_=SQ2[ii].rearrange("p (c d) -> p c d", c=4), axis=AX.X),
                      reads=[bSQ2[ii]], writes=[bST[ii]])
                P.dve(lambda e: e.tensor_scalar(out=s1, in0=s1, scalar1=1.0 / 128, scalar2=None, op0=ALU.mult),
                      reads=[bST[ii]], writes=[bST[ii]])
                P.dve(lambda e: e.tensor_tensor(out=s3, in0=s1, in1=s1, op=ALU.mult),
                      reads=[bST[ii]], writes=[bST[ii]])
                P.dve(lambda e: e.scalar_tensor_tensor(out=s2, in0=s2, scalar=1.0 / 128, in1=s3, op0=ALU.mult, op1=ALU.subtract),
                      reads=[bST[ii]], writes=[bST[ii]])
                P.dve(lambda e: e.tensor_scalar(out=s2, in0=s2, scalar1=EPS, scalar2=None, op0=ALU.add),
                      reads=[bST[ii]], writes=[bST[ii]])
                yield
                P.act(lambda e: e.activation(out=s2, in_=s2, func=AF.Ln), reads=[bST[ii]], writes=[bST[ii]])
                P.act(lambda e: e.activation(out=s2, in_=s2, func=AF.Exp, scale=-0.5), reads=[bST[ii]], writes=[bST[ii]])
                yield
                P.dve(lambda e: e.scalar_tensor_tensor(out=s4, in0=s1, scalar=-1.0, in1=s2, op0=ALU.mult, op1=ALU.mult),
                      reads=[bST[ii]], writes=[bST[ii]])
                for j in range(4):
                    P.act(lambda e, j=j: e.activation(out=YN[ii][:, j * 128:(j + 1) * 128], in_=PS[ob][:, j * 128:(j + 1) * 128],
                                                      func=AF.Identity, bias=s4[:, j:j + 1], scale=s2[:, j:j + 1]),
                          reads=[bPS[ob], bST[ii]], writes=[bYN[ii]])
                yield
                for j in range(4):
                    P.pe(lambda e, j=j: e.transpose(PS[ib][:, j * 128:(j + 1) * 128], YN[ii][:, j * 128:(j + 1) * 128], IDN[:]),
                         reads=[bYN[ii], bC['idn']], writes=[bPS[ib]])
                sl = slice(g4 * 512, (g4 + 1) * 512)
                P.dve(lambda e, sl=sl: e.scalar_tensor_tensor(out=yT[:, sl], in0=PS[ib][:, :], scalar=GNG[:, l, h:h + 1],
                                                              in1=SZ[:, sl], op0=ALU.mult, op1=ALU.mult),
                      reads=[bPS[ib], bC['gng'], bsz], writes=[byT])
            for pair in ((0, 1, 2, 3),):
                gens = [out_group(g) for g in pair]
                while gens:
                    for g_ in list(gens):
                        try:
                            next(g_)
                        except StopIteration:
                            gens.remove(g_)
            store_y(yT, byT, h)

        nxt = ret_load(0)
        for h in range(NH):
            cur = nxt
            if h + 1 < NH:
                nxt = ret_load(h + 1)
            ret_head(h, cur[0], cur[1])

    def phase_pool(l):
        o = OFF_B
        U = [arf(o, 2048), arf(o + 2048, 2048)]
        o += 4096
        SA = arf(o, 2048)
        SBb = arf(o + 2048, 2048)
        o += 4096
        SZ = [arf(o, 2048), arf(o + 2048, 2048)]
        o += 4096
        PT = [arb(o, 1024), arb(o + 1024, 1024)]
        o += 2048
        yT = arb(o, 1024)
        o += 1024
        T16 = arf(o, 16)
        o += 16
        assert o <= 24576
        bU, bSZ, bPT = [Buf(), Buf()], [Buf(), Buf()], [Buf(), Buf()]
        bSA, bSB, byT, bT16 = Buf(), Buf(), Buf(), Buf()
        pstg = Stager(0, 3)

        def pool_load(g):
            a, b_ = 2 * g, 2 * g + 1
            return load_w4s(pstg, l, [PU + a * 128, PU + b_ * 128, PZ + a * 128, PZ + b_ * 128])

        def do_group(g, wi, wv):
            w = POOL_W[g]
            a, b_ = 2 * g, 2 * g + 1
            src = pool_w[l, g].rearrange("(c p) d -> p c d", p=128)
            P.dma('pool', PW[:], src, writes=[bC['pw']])
            for ci in range(2):
                def ev_u(tt, ps, bps, ci=ci):
                    evac_copy(tt, U[ci][:, tt * 512:(tt + 1) * 512], ps, [bps], [bU[ci]])

                def ev_z(tt, ps, bps, ci=ci):
                    P.act(lambda e: e.activation(out=SZ[ci][:, tt * 512:(tt + 1) * 512], in_=ps, func=AF.Silu),
                          reads=[bps], writes=[bSZ[ci]])
                proj(wi, wv, ci, ev_u)
                proj(wi, wv, 2 + ci, ev_z)
            for ci in range(2):
                cur, bcur = U[ci], bU[ci]
                nxt = [(SA, bSA), (SBb, bSB)]
                step = 1
                k = 0
                while step < w:
                    dst, bdst = nxt[k % 2]
                    eng = P.pool if k % 2 == 0 else P.dve
                    eng(lambda e, dst=dst, cur=cur, step=step: e.tensor_tensor(out=dst[:, step:S], in0=cur[:, step:S],
                                                                                in1=cur[:, 0:S - step], op=ALU.add),
                        reads=[bcur], writes=[bdst])
                    eng(lambda e, dst=dst, cur=cur, step=step: e.tensor_copy(out=dst[:, 0:step], in_=cur[:, 0:step]),
                        reads=[bcur], writes=[bdst])
                    cur, bcur = dst, bdst
                    step *= 2
                    k += 1
                P.dve(lambda e, cur=cur, ci=ci: e.scalar_tensor_tensor(out=PT[ci], in0=cur, scalar=1.0 / w, in1=U[ci],
                                                                       op0=ALU.mult, op1=ALU.subtract),
                      reads=[bcur, bU[ci]], writes=[bPT[ci]])
                P.dve(lambda e, cur=cur: e.tensor_tensor(out=T16, in0=cur[:, 0:16], in1=R16[:, g, :], op=ALU.mult),
                      reads=[bcur, bC['r16']], writes=[bT16])
                P.dve(lambda e, ci=ci: e.tensor_tensor(out=PT[ci][:, 0:16], in0=T16, in1=U[ci][:, 0:16], op=ALU.subtract),
                      reads=[bT16, bU[ci], bPT[ci]], writes=[bPT[ci]])
            for dd in range(2):
                for tt in range(4):
                    pb = 4 + tt
                    for cc in range(2):
                        P.pe(lambda e, cc=cc, tt=tt, dd=dd, pb=pb: e.matmul(
                            PS[pb][:, :], lhsT=PW[:, cc, dd * 128:(dd + 1) * 128], rhs=PT[cc][:, tt * 512:(tt + 1) * 512],
                            start=(cc == 0), stop=(cc == 1)),
                            reads=[bC['pw'], bPT[cc]], writes=[bPS[pb]])
                    sl = slice(tt * 512, (tt + 1) * 512)
                    P.dve(lambda e, sl=sl, dd=dd, pb=pb: e.scalar_tensor_tensor(
                        out=yT[:, sl], in0=PS[pb][:, :], scalar=PSC[:, l, 2 * g + dd:2 * g + dd + 1], in1=SZ[dd][:, sl],
                        op0=ALU.mult, op1=ALU.mult),
                        reads=[bPS[pb], bC['psc'], bSZ[dd]], writes=[byT])
                store_y(yT, byT, 16 + 2 * g + dd)
        nxt = pool_load(0)
        for g in range(4):
            cur = nxt
            if g + 1 < 4:
                nxt = pool_load(g + 1)
            do_group(g, cur[0], cur[1])

    def phase_c(l):
        YH = arb(0, 12288).rearrange("p (k t) -> p k t", k=24)
        bYH = [Buf() for _ in range(24)]
        o = 12288
        SG = [[arf(o + (i * 3 + j) * 512, 512) for j in range(3)] for i in range(2)]
        bSG = [[Buf() for j in range(3)] for i in range(2)]
        o += 3072
        ACC = [arf(o, 512), arf(o + 512, 512)]
        TMP = [arf(o + 1024, 512), arf(o + 1536, 512)]
        bACC, bTMP = [Buf(), Buf()], [Buf(), Buf()]
        o += 2048
        MT = [arb(o, 512), arb(o + 512, 512)]
        bMT = [Buf(), Buf()]
        o += 1024
        assert o <= 24576
        branch_w = (w_rb, w_fb, w_pb)
        itc = [0]

        cstg = Stager(18432, 3)

        def c_load(dc):
                i = wb_next()
                gv = WB[i][:, 0:6144].rearrange("p (k c) -> p k c", k=KC)
                bv_ = WB[i][:, 6144:9216].rearrange("p (k c) -> p k c", k=24)
                for j, c0 in enumerate((GA, GB, GC)):
                    src = w_in[l, :, c0 + dc * 128:c0 + (dc + 1) * 128].rearrange("(k p) c -> p k c", p=128)
                    cstg.load(gv[:, :, j * 128:(j + 1) * 128], src, KC, [bWB[i]])
                for j in range(3):
                    src = branch_w[j][l, :, dc * 128:(dc + 1) * 128].rearrange("(k p) c -> p k c", p=128)
                    cstg.load(bv_[:, j * 8:(j + 1) * 8, :], src, 8, [bWB[i]])
                return i, gv, bv_

        def do_dc(half, dc, t0, i, gv, bv_):
                mi = dc % 2
                for t2 in range(2):
                    tt = half * 2 + t2
                    si = itc[0] % 2
                    itc[0] += 1
                    gb0 = 0 if si == 0 else 3
                    for j in range(3):
                        pb = gb0 + j
                        for k in range(KC):
                            P.pe(lambda e, k=k, j=j, pb=pb, tt=tt: e.matmul(
                                PS[pb][:, :], lhsT=gv[:, k, j * 128:(j + 1) * 128], rhs=HT[:, k, tt * 512:(tt + 1) * 512],
                                start=(k == 0), stop=(k == KC - 1)),
                                reads=[bWB[i]] + bHT[tt], writes=[bPS[pb]])
                        P.act(lambda e, j=j, pb=pb, si=si: e.activation(out=SG[si][j], in_=PS[pb][:, :], func=AF.Sigmoid),
                              reads=[bPS[pb]], writes=[bSG[si][j]])
                    for j in range(3):
                        pb = 6 + (j % 2)
                        for k in range(8):
                            P.pe(lambda e, k=k, j=j, pb=pb, t2=t2: e.matmul(
                                PS[pb][:, :], lhsT=bv_[:, j * 8 + k, :], rhs=YH[:, j * 8 + k, t2 * 512:(t2 + 1) * 512],
                                start=(k == 0), stop=(k == 7)),
                                reads=[bWB[i], bYH[j * 8 + k]], writes=[bPS[pb]])
                        if j == 0:
                            P.dve(lambda e, pb=pb, si=si: e.tensor_tensor(out=ACC[si], in0=PS[pb][:, :], in1=SG[si][0], op=ALU.mult),
                                  reads=[bPS[pb], bSG[si][0]], writes=[bACC[si]])
                        else:
                            P.dve(lambda e, pb=pb, si=si, j=j: e.tensor_tensor(out=TMP[si], in0=PS[pb][:, :], in1=SG[si][j], op=ALU.mult),
                                  reads=[bPS[pb], bSG[si][j]], writes=[bTMP[si]])
                            if j == 1:
                                P.dve(lambda e, si=si: e.tensor_tensor(out=ACC[si], in0=ACC[si], in1=TMP[si], op=ALU.add),
                                      reads=[bACC[si], bTMP[si]], writes=[bACC[si]])
                            else:
                                P.dve(lambda e, si=si, t2=t2, mi=mi: e.tensor_tensor(out=MT[mi][:, t2 * 512:(t2 + 1) * 512],
                                                                                     in0=ACC[si], in1=TMP[si], op=ALU.add),
                                      reads=[bACC[si], bTMP[si]], writes=[bMT[mi]])
                P.dma('sp', Md[dc * 128:(dc + 1) * 128, t0:t0 + 1024], MT[mi], reads=[bMT[mi]], writes=[bMd[dc][half]])

        cnxt = [c_load(0)]
        for half in range(2):
            t0 = half * 1024
            for k in range(24):
                P.dma('sp', YH[:, k, :], Yd[k * 128:(k + 1) * 128, t0:t0 + 1024], reads=[bYd[k]], writes=[bYH[k]])
            for dc in range(16):
                cur = cnxt[0]
                nd = half * 16 + dc + 1
                if nd < 32:
                    cnxt[0] = c_load(nd % 16)
                do_dc(half, dc, t0, cur[0], cur[1], cur[2])

    def phase_d(l, xsrc, bxsrc, xdst, bxdst, final):
        MTT = [arb(0, 4096).rearrange("p (k t) -> p k t", k=KC), arb(4096, 4096).rearrange("p (k t) -> p k t", k=KC)]
        bMTT = [Buf(), Buf()]
        XI_ = [arf(8192, 2048), arf(10240, 2048)]
        XO = [arf(12288, 2048), arf(14336, 2048)]
        FG = arf(16384, 2048)
        SQ = arf(18432, 2048)
        bXI, bXO = [Buf(), Buf()], [Buf(), Buf()]
        bFG, bSQ = Buf(), Buf()
        dstg = Stager(20480, 2)
        for c16 in range(16):
            src = w_out[l, :, c16 * 128:(c16 + 1) * 128].rearrange("(k p) c -> p k c", p=128)
            dstg.load(HT[:, :, c16 * 128:(c16 + 1) * 128], src, KC, bHT[c16 // 4])
        if final:
            P.dma('sp', FG, final_g.broadcast_to([128, D]), writes=[bFG])
        for tt in range(4):
            mi = tt % 2
            for dc in range(16):
                P.dma('sp', MTT[mi][:, dc, :], Md[dc * 128:(dc + 1) * 128, tt * 512:(tt + 1) * 512],
                      reads=[bMd[dc][tt // 2]], writes=[bMTT[mi]])
            for t4 in range(4):
                t = tt * 4 + t4
                xi_ = t % 2
                P.dma('act', XI_[xi_], xsrc[t * 128:(t + 1) * 128, :], reads=([bxsrc[t]] if bxsrc else []), writes=[bXI[xi_]])
                pbase = 0 if t % 2 == 0 else 4
                for cb in range(4):
                    for k in range(KC):
                        P.pe(lambda e, k=k, cb=cb, t4=t4, mi=mi, pbase=pbase: e.matmul(
                            PS[pbase + cb][:, :], lhsT=MTT[mi][:, k, t4 * 128:(t4 + 1) * 128],
                            rhs=HT[:, k, cb * 512:(cb + 1) * 512], start=(k == 0), stop=(k == KC - 1)),
                            reads=[bMTT[mi]] + bHT[cb], writes=[bPS[pbase + cb]])
                for cb in range(4):
                    sl = slice(cb * 512, (cb + 1) * 512)
                    P.dve(lambda e, cb=cb, sl=sl, xi_=xi_, pbase=pbase: e.tensor_tensor(
                        out=XO[xi_][:, sl], in0=PS[pbase + cb][:, :], in1=XI_[xi_][:, sl], op=ALU.add),
                        reads=[bPS[pbase + cb], bXI[xi_]], writes=[bXO[xi_]])
                if final:
                    rstd = rms_stats(XO[xi_], bXO[xi_], SQ, bSQ, 8 + 4 * xi_)
                    P.dve(lambda e, xi_=xi_, rstd=rstd: e.scalar_tensor_tensor(out=XO[xi_], in0=XO[xi_], scalar=rstd, in1=FG,
                                                                               op0=ALU.mult, op1=ALU.mult),
                          reads=[bXO[xi_], bC['stat'], bFG], writes=[bXO[xi_]])
                fin.append(P.dma('sp', xdst[t * 128:(t + 1) * 128, :], XO[xi_], reads=[bXO[xi_]],
                                 writes=([bxdst[t]] if bxdst else [])))

    fin = []
    for l in range(n_layers):
        last = (l == n_layers - 1)
        xsrc, bxsrc = (x_in, None) if l == 0 else (x1, bX1)
        xdst, bxdst = (out, None) if last else (x1, bX1)
        phase_a(l, xsrc, bxsrc)
        P.barrier()
        if upto == 'a':
            break
        if upto != 'nofox':
            phase_fox(l)
        P.barrier()
        if upto == 'fox':
            break
        P.dma('sp', COS, ccos, writes=[bC['cos']])
        P.dma('sp', SIN, csin, writes=[bC['sin']])
        P.dma('sp', DEC, cdecay, writes=[bC['decay']])
        P.dma('sp', XI, cxi, writes=[bC['xi']])
        phase_ret(l)
        P.barrier()
        if upto in ('ret', 'nofox'):
            break
        phase_pool(l)
        P.barrier()
        if upto == 'pool':
            break
        phase_c(l)
        P.barrier()
        if upto == 'c':
            break
        del fin[:]
        phase_d(l, xsrc, bxsrc, xdst, bxdst, final=(last and not debug))
        P.barrier()
    fw = list(fin)
    P.emit(final_waits=fw)
    return nc, consts


_CACHE = {}


def _prep_inputs(inputs):
    f = lambda a: np.ascontiguousarray(np.asarray(a, dtype=np.float32))
    shared = {
        'norm_g': f(np.asarray(inputs['norm_g']).reshape(2, KC, 128).transpose(0, 2, 1)),
        'w_in': f(inputs['w_in']),
        'ret_gn_g': f(np.asarray(inputs['ret_gn_g']).reshape(2, NH, 128).transpose(0, 2, 1)),
        'fox_b_f': f(np.asarray(inputs['fox_b_f']).reshape(2, NH).T),
        'pool_w': f(inputs['pool_w']),
        'pool_scale': f(np.asarray(inputs['pool_scale']).reshape(2, 8, 128).transpose(0, 2, 1)),
        'w_ret_branch': f(inputs['w_ret_branch']),
        'w_fox_branch': f(inputs['w_fox_branch']),
        'w_pool_branch': f(inputs['w_pool_branch']),
        'w_out': f(inputs['w_out']),
        'final_g': f(np.asarray(inputs['final_g']).reshape(1, D)),
    }
    return shared


def kernel(**inputs):
    if 'nc' not in _CACHE:
        _CACHE['nc'] = build(2, False)
    nc, consts = _CACHE['nc']
    shared = _prep_inputs(inputs)
    shared.update(consts)
    x = np.asarray(inputs['x'], dtype=np.float32)
    in_maps = []
    for b in range(8):
        m = dict(shared)
        m['x'] = np.ascontiguousarray(x[b])
        in_maps.append(m)
    res = run_bass_kernel_spmd(nc, in_maps, core_ids=list(range(8)))
    return np.stack([np.asarray(r['out'], dtype=np.float32) for r in res.results], axis=0)
```
